# Optimizing a Trainium2 kernel written in Bass

```python
import jax, jax.numpy as jnp
from jax import lax
import numpy as np

D_MODEL = 1024
BATCH = 16
SEQ = 2048
DEPTH = 4

CHUNK = 64
Q_BLOCK = 2 * CHUNK
CONV_WIDTH = 31
D_CONV = D_MODEL // 2
SB_HEADS = 8
SB_HEAD_DIM = 64
D_SB = SB_HEADS * SB_HEAD_DIM
EPS = 1e-6
IN_SIZES = (2 * D_CONV, D_CONV, D_SB, D_SB, D_SB, D_SB, D_MODEL, D_MODEL)
D_IN = sum(IN_SIZES)
IN_SPLITS = tuple(int(i) for i in np.cumsum(IN_SIZES)[:-1])

kernel_name = "hybrid_conformer_conv_stickbreaking_gated"


def rms_norm(x, g):
    xf = x.astype(jnp.float32)
    y = xf * lax.rsqrt(jnp.mean(xf * xf, axis=-1, keepdims=True) + EPS)
    return (y * g.astype(jnp.float32)).astype(x.dtype)


def layer_norm(x, g, b):
    xf = x.astype(jnp.float32)
    mu = jnp.mean(xf, axis=-1, keepdims=True)
    xc = xf - mu
    var = jnp.mean(xc * xc, axis=-1, keepdims=True)
    y = xc * lax.rsqrt(var + EPS) * g.astype(jnp.float32) + b.astype(jnp.float32)
    return y.astype(x.dtype)


def conformer_conv_branch(glu_in, gate, w_dw, b_dw, ln_g, ln_b, w_proj):
    a, b = jnp.split(glu_in, 2, axis=-1)
    u = a * jax.nn.sigmoid(b)
    u = lax.conv_general_dilated(
        u, w_dw[:, None, :].astype(u.dtype), window_strides=(1,),
        padding=[(CONV_WIDTH - 1, 0)],
        dimension_numbers=("NWC", "WIO", "NWC"),
        feature_group_count=D_CONV) + b_dw
    u = layer_norm(u, ln_g, ln_b)
    u = jax.nn.silu(u) * jax.nn.silu(gate)
    return u @ w_proj


def stick_breaking_attention(q, k, v):
    scale = SB_HEAD_DIM ** -0.5
    seq = q.shape[1]
    outs = []
    for start in range(0, seq, Q_BLOCK):
        end = start + Q_BLOCK
        qb = q[:, start:end]
        kb = k[:, :end]
        vb = v[:, :end]
        z = jnp.einsum("bqhd,bkhd->bhqk", qb, kb,
                       preferred_element_type=jnp.float32) * scale
        t_idx = start + jnp.arange(Q_BLOCK)[:, None]
        s_idx = jnp.arange(end)[None, :]
        mask = s_idx < t_idx
        log_beta = jax.nn.log_sigmoid(z)
        log_1m_beta = jnp.where(mask, log_beta - z, 0.0)
        suffix = lax.cumsum(log_1m_beta, axis=3, reverse=True) - log_1m_beta
        weights = jnp.where(mask, jnp.exp(log_beta + suffix), 0.0)
        outs.append(jnp.einsum("bhqk,bkhd->bqhd", weights.astype(v.dtype), vb))
    return jnp.concatenate(outs, axis=1)


def head_rms_norm(x, g):
    xf = x.astype(jnp.float32)
    y = xf * lax.rsqrt(jnp.mean(xf * xf, axis=-1, keepdims=True) + EPS)
    return (y * g.astype(jnp.float32)).astype(x.dtype)


def setup_inputs(seed: int = 0) -> dict:
    key = jax.random.key(seed)
    ks = jax.random.split(key, 16)
    f32 = jnp.float32
    x = jax.random.normal(ks[0], (BATCH, SEQ, D_MODEL), f32)
    c = jax.random.normal(ks[1], (BATCH, D_MODEL), f32)
    w_ada = jax.random.normal(ks[2], (DEPTH, D_MODEL, 3 * D_MODEL), f32) * (0.1 * D_MODEL ** -0.5)
    b_small = 0.02 * jax.random.normal(ks[3], (DEPTH, 3 * D_MODEL), f32)
    b_ada = b_small + jnp.concatenate(
        [jnp.zeros((DEPTH, 2 * D_MODEL), f32), jnp.ones((DEPTH, D_MODEL), f32)], axis=-1)
    norm_g = 1.0 + 0.02 * jax.random.normal(ks[4], (DEPTH, D_MODEL), f32)
    w_in = jax.random.normal(ks[5], (DEPTH, D_MODEL, D_IN), f32) * D_MODEL ** -0.5
    q_gain = 1.0 + 0.02 * jax.random.normal(ks[6], (DEPTH, SB_HEAD_DIM), f32)
    k_gain = 1.0 + 0.02 * jax.random.normal(ks[7], (DEPTH, SB_HEAD_DIM), f32)
    w_dw = jax.random.normal(ks[8], (DEPTH, CONV_WIDTH, D_CONV), f32) * CONV_WIDTH ** -0.5
    b_dw = 0.02 * jax.random.normal(ks[9], (DEPTH, D_CONV), f32)
    ln_g = 1.0 + 0.02 * jax.random.normal(ks[10], (DEPTH, D_CONV), f32)
    ln_b = 0.02 * jax.random.normal(ks[11], (DEPTH, D_CONV), f32)
    w_conv_out = jax.random.normal(ks[12], (DEPTH, D_CONV, D_MODEL), f32) * D_CONV ** -0.5
    w_sb_out = jax.random.normal(ks[13], (DEPTH, D_SB, D_MODEL), f32) * D_SB ** -0.5
    w_out = jax.random.normal(ks[14], (DEPTH, D_MODEL, D_MODEL), f32) * D_MODEL ** -0.5
    return {"x": x, "c": c, "w_ada": w_ada, "b_ada": b_ada, "norm_g": norm_g, "w_in": w_in,
            "q_gain": q_gain, "k_gain": k_gain, "w_dw": w_dw, "b_dw": b_dw, "ln_g": ln_g,
            "ln_b": ln_b, "w_conv_out": w_conv_out, "w_sb_out": w_sb_out, "w_out": w_out}


def reference(x, c, w_ada, b_ada, norm_g, w_in, q_gain, k_gain, w_dw, b_dw, ln_g, ln_b,
              w_conv_out, w_sb_out, w_out):
    bsz, seq, _ = x.shape
    c_act = jax.nn.silu(c)
    for l in range(DEPTH):
        mod = c_act @ w_ada[l] + b_ada[l]
        shift, scale, gate = jnp.split(mod[:, None, :], 3, axis=-1)
        h = rms_norm(x, norm_g[l]) * (1.0 + scale) + shift
        u = h @ w_in[l]
        glu_in, gate_a, q, k, v, gate_b, m_a, m_b = jnp.split(u, IN_SPLITS, axis=-1)
        y_a = conformer_conv_branch(glu_in, gate_a, w_dw[l], b_dw[l], ln_g[l], ln_b[l], w_conv_out[l])
        q = head_rms_norm(q.reshape(bsz, seq, SB_HEADS, SB_HEAD_DIM), q_gain[l])
        k = head_rms_norm(k.reshape(bsz, seq, SB_HEADS, SB_HEAD_DIM), k_gain[l])
        v = v.reshape(bsz, seq, SB_HEADS, SB_HEAD_DIM)
        o = stick_breaking_attention(q, k, v).reshape(bsz, seq, D_SB)
        y_b = (o * jax.nn.silu(gate_b)) @ w_sb_out[l]
        y = jax.nn.sigmoid(m_a) * y_a + jax.nn.sigmoid(m_b) * y_b
        x = x + gate * (y @ w_out[l])
    return x
```

```python
import contextlib
import numpy as np
import concourse.bass as bass
import concourse.mybir as mybir
from concourse.bass_utils import run_bass_kernel_spmd

F32 = mybir.dt.float32
BF16 = mybir.dt.bfloat16
AF = mybir.ActivationFunctionType
ALU = mybir.AluOpType

D = 1024
S = 2048
KC = 8
NL = 4
DIN = 5632
EPS = 1e-6
NCORES = 8
NSEQ = 2
VL = 24 + 8 + 1 + 1 + 4 * 31 + 4 + 4 + 4
NCST = 128 * 5 + 256 + 128
NSLOT = 6
NF512 = 6


class Sched:
    def __init__(self):
        self.q = {e: [] for e in ("pe", "act", "dve", "pool", "sp")}
        self.cnt = {}
        self.lastw = {}
        self.rd = {}

    def _deps(self, reads, writes, eng):
        deps = {}

        def add(tok):
            if tok is None:
                return
            s, v = tok
            if s == "pe" and eng == "pe":
                return
            if deps.get(s, 0) < v:
                deps[s] = v

        for r in reads:
            add(self.lastw.get(r))
        for w in writes:
            add(self.lastw.get(w))
            for s, v in self.rd.get(w, {}).items():
                add((s, v))
        return deps

    def _commit(self, reads, writes, tok):
        for r in reads:
            d = self.rd.setdefault(r, {})
            if d.get(tok[0], 0) < tok[1]:
                d[tok[0]] = tok[1]
        for w in writes:
            self.lastw[w] = tok
            self.rd[w] = {}

    def op(self, eng, fn, reads=(), writes=(), inc=True):
        deps = self._deps(reads, writes, eng)
        nxt = self.cnt.get(eng, 0) + 1
        if inc:
            self.cnt[eng] = nxt
        self._commit(reads, writes, (eng, nxt))
        self.q[eng].append((deps, fn, eng if inc else None, 1))

    def dma(self, qeng, dsem, fn, reads=(), writes=(), extra=None):
        deps = self._deps(reads, writes, qeng)
        if extra:
            for k_, v_ in extra.items():
                if v_ > deps.get(k_, 0):
                    deps[k_] = v_
        self.cnt[dsem] = self.cnt.get(dsem, 0) + 16
        self._commit(reads, writes, (dsem, self.cnt[dsem]))
        self.q[qeng].append((deps, fn, dsem, 16))

    def fence(self, engines=("pe", "act", "dve", "pool")):
        snap = {k: v for k, v in self.cnt.items() if k in engines}
        for e in engines:
            self.q[e].append((dict(snap), None, None, 0))

    def wait_all(self, eng):
        self.q[eng].append((dict(self.cnt), None, None, 0))


import os
DBG = os.environ.get("KDBG", "")


def build(n_layers, nseq):
    nc = bass.Bass("TRN2", target_bir_lowering=False)
    NV = n_layers * VL
    xT_d = nc.dram_tensor("xT", [nseq, KC, 128, S], F32, kind="ExternalInput").ap()
    yT_d = nc.dram_tensor("yT", [nseq, KC, 128, S], F32, kind="ExternalOutput").ap()
    cT_d = nc.dram_tensor("cT", [128, KC * nseq], F32, kind="ExternalInput").ap()
    vec_d = nc.dram_tensor("vecs", [128, NV], F32, kind="ExternalInput").ap()
    cst_d = nc.dram_tensor("csts", [128, NCST], F32, kind="ExternalInput").ap()
    wada_d = nc.dram_tensor("wada", [n_layers, 24, 128, KC * 128], F32, kind="ExternalInput").ap()
    win_d = nc.dram_tensor("win", [n_layers, 44, 128, KC * 128], F32, kind="ExternalInput").ap()
    wco_d = nc.dram_tensor("wco", [n_layers, 8, 128, 4 * 128], F32, kind="ExternalInput").ap()
    wso_d = nc.dram_tensor("wso", [n_layers, 8, 128, 4 * 128], F32, kind="ExternalInput").ap()
    wo_d = nc.dram_tensor("wo", [n_layers, 8, 128, 8 * 128], F32, kind="ExternalInput").ap()

    wbf_d = {
        "in": nc.dram_tensor("wbf_in", [n_layers, 44, 128, KC * 128], BF16, kind="Internal").ap(),
        "co": nc.dram_tensor("wbf_co", [n_layers, 8, 128, 4 * 128], BF16, kind="Internal").ap(),
        "so": nc.dram_tensor("wbf_so", [n_layers, 8, 128, 4 * 128], BF16, kind="Internal").ap(),
        "o": nc.dram_tensor("wbf_o", [n_layers, 8, 128, 8 * 128], BF16, kind="Internal").ap(),
    }
    wf32_d = {"in": win_d, "co": wco_d, "so": wso_d, "o": wo_d}
    sch = Sched()
    es = contextlib.ExitStack()

    def sb(name, shape, dt):
        return es.enter_context(nc.sbuf_tensor("sb_" + name, shape, dt))

    xT = sb("xT", [128, KC, S], F32)
    hT = sb("hT", [128, KC, S], BF16)
    oT = sb("oT", [128, 4, S], BF16)
    vecs = sb("vecs", [128, NV], F32)
    cst = sb("cst", [128, NCST], BF16)
    cact = sb("cact", [128, KC * nseq], F32)
    cin = sb("cin", [128, KC * nseq], F32)
    modT = sb("modT", [128, n_layers * 24 * nseq], F32)
    g1 = sb("g1", [128, n_layers * 8 * nseq], F32)
    qg8 = sb("qg8", [128, n_layers], F32)
    ring = [sb(f"ring{i}", [128, 1024], BF16) for i in range(NSLOT)]
    f512 = [sb(f"f512_{i}", [128, 512], F32) for i in range(NF512)]
    sqb = [sb(f"sq{i}", [128, 512], BF16) for i in range(4)]
    rstdN = sb("rstdN", [128, 512], F32)
    UBYTES = 58 * 1024
    U = sb("U", [128, UBYTES // 4], F32)

    TRI = cst[:, 0:128]
    NONES = cst[:, 128:256]
    BD64 = cst[:, 256:384]
    ON1024 = cst[:, 384:512]
    ON512 = cst[:, 512:640]
    MASK2 = cst[:, 640:896]
    IDENT = cst[:, 896:1024]

    Z = [es.enter_context(nc.psum_tensor(f"Z{i}", [128, 2, 512], F32)) for i in range(2)]
    Ob = [es.enter_context(nc.psum_tensor(f"O{i}", [128, 512], F32)) for i in range(2)]
    Pb = [es.enter_context(nc.psum_tensor(f"P{i}", [128, 512], F32)) for i in range(2)]
    banks = [Z[0][:, 0, :], Z[0][:, 1, :], Z[1][:, 0, :], Z[1][:, 1, :], Ob[0][:], Ob[1][:], Pb[0][:], Pb[1][:]]
    bank_ctr = [0, 0]

    def next_bank(attn=False):
        if attn:
            i = 6 + (bank_ctr[1] % 2)
            bank_ctr[1] += 1
        else:
            i = bank_ctr[0] % 8
            bank_ctr[0] += 1
        return banks[i], ("ps", i)

    rot_ctr = {"f": 0, "sq": 0, "ring": 0}

    def rf():
        i = rot_ctr["f"] % NF512
        rot_ctr["f"] += 1
        return f512[i], ("f512", i)

    def rsq():
        i = rot_ctr["sq"] % 4
        rot_ctr["sq"] += 1
        return sqb[i], ("sq", i)

    def mm(out, lhsT, rhs, start, stop, reads, writes, inc=True, **kw):
        sch.op("pe", lambda e: e.matmul(out, lhsT, rhs, start=start, stop=stop, **kw), reads, writes, inc)

    def act(out, in_, func, reads, writes, **kw):
        sch.op("act", lambda e: e.activation(out=out, in_=in_, func=func, **kw), reads, writes)

    def tt(eng, out, in0, in1, op, reads, writes):
        sch.op(eng, lambda e: e.tensor_tensor(out=out, in0=in0, in1=in1, op=op), reads, writes)

    def ts(eng, out, in0, s1, s2, op0, op1, reads, writes):
        if s2 is None:
            sch.op(eng, lambda e: e.tensor_scalar(out=out, in0=in0, scalar1=s1, scalar2=None, op0=op0), reads, writes)
        else:
            sch.op(eng, lambda e: e.tensor_scalar(out=out, in0=in0, scalar1=s1, scalar2=s2, op0=op0, op1=op1), reads, writes)

    def stt(out, in0, scalar, in1, op0, op1, reads, writes):
        sch.op("dve", lambda e: e.scalar_tensor_tensor(out=out, in0=in0, scalar=scalar, in1=in1, op0=op0, op1=op1), reads, writes)

    def cp(eng, out, in_, reads, writes):
        sch.op(eng, lambda e: e.tensor_copy(out=out, in_=in_), reads, writes)

    converted = set()

    def cvgrp(kind, j):
        return "A" if (kind == "in" and 12 <= j < 24) else "B"

    def convert(kind, li, j):
        if (kind, li, j) in converted or li >= n_layers:
            return
        converted.add((kind, li, j))
        src = wf32_d[kind][li, j]
        dst = wbf_d[kind][li, j]
        sch.dma("pool", f"cv{cvgrp(kind, j)}{li}", lambda e: e.dma_start(out=dst, in_=src), (), ())

    def wpiece(kind, li, j, ncols):
        convert(kind, li + 1, j)
        i = rot_ctr["ring"] % NSLOT
        rot_ctr["ring"] += 1
        dst = ring[i][:, 0:ncols]
        src = wbf_d[kind][li, j]
        cs_ = f"cv{cvgrp(kind, j)}{li}"
        sch.dma("sp", f"ring{i}", lambda e: e.dma_start(out=dst, in_=src), (), [("ring", i)],
                extra={cs_: sch.cnt[cs_]})
        return ring[i], ("ring", i)

    def proj8(pb, pr, w, wr, tc):
        for kc in range(KC):
            mm(pb, w[:, kc * 128:(kc + 1) * 128], hT[:, kc, tc * 512:(tc + 1) * 512], kc == 0, kc == KC - 1,
               [wr, ("h", kc, tc)], [pr], inc=(kc == KC - 1))

    def vcol(li, off, n=1):
        c = li * VL + off
        return vecs[:, c:c + n]

    sch.dma("sp", "ldv", lambda e: e.dma_start(out=vecs[:], in_=vec_d), (), ["vecs"])
    sch.dma("pool", "ldc", lambda e: e.dma_start(out=cst[:], in_=cst_d), (), ["cst"])
    for j in list(range(12, 24)) + list(range(0, 12)) + list(range(24, 44)):
        convert("in", 0, j)
    for j in range(8):
        convert("co", 0, j)
        convert("so", 0, j)
    for j in range(8):
        convert("o", 0, j)
    sch.dma("sp", "ldcin", lambda e: e.dma_start(out=cin[:], in_=cT_d), (), ["cin"])
    e0, e0r = rf()
    act(e0[:, 0:KC * nseq], cin[:], AF.Exp, ["cin"], [e0r], scale=-1.0)
    ts("dve", e0[:, 0:KC * nseq], e0[:, 0:KC * nseq], 1.0, None, ALU.add, None, [e0r], [e0r])
    sch.op("dve", lambda e: e.reciprocal(out=e0[:, 0:KC * nseq], in_=e0[:, 0:KC * nseq]), [e0r], [e0r])
    tt("dve", cact[:], cin[:], e0[:, 0:KC * nseq], ALU.mult, ["cin", e0r], ["cact"])
    wad = [U[:, 0:1024], U[:, 1024:2048]]
    cact3 = cact[:]
    for li in range(n_layers):
        for j in range(24):
            wi = (li * 24 + j) % 2
            wsrc = wada_d[li, j]
            wdst = wad[wi]
            sch.dma("sp", f"wad{wi}", lambda e, wdst=wdst, wsrc=wsrc: e.dma_start(out=wdst, in_=wsrc), (), [("wad", wi)])
            pb, pr = next_bank()
            for kc in range(KC):
                mm(pb[:, 0:nseq], wdst[:, kc * 128:(kc + 1) * 128], cact[:, kc * nseq:(kc + 1) * nseq],
                   kc == 0, kc == KC - 1, [("wad", wi), "cact"], [pr], inc=(kc == KC - 1))
            c0 = (li * 24 + j) * nseq
            ts("dve", modT[:, c0:c0 + nseq], pb[:, 0:nseq], vcol(li, j), None, ALU.add, None, [pr, "vecs"], ["modT"])
        for kc in range(KC):
            c0 = (li * 24 + 8 + kc) * nseq
            gc = (li * 8 + kc) * nseq
            ts("dve", g1[:, gc:gc + nseq], modT[:, c0:c0 + nseq], 1.0, vcol(li, 24 + kc), ALU.add, ALU.mult,
               ["modT", "vecs"], ["g1"])
        ts("dve", qg8[:, li:li + 1], vcol(li, 32), 0.125, None, ALU.mult, None, ["vecs"], ["qg8"])
    sch.fence()

    def ubf(off_bytes, shape):
        n = int(np.prod(shape))
        t = U[:, off_bytes // 4: off_bytes // 4 + n // 2].bitcast(BF16)
        return t

    KB = 1024
    qTb = [ubf(0, [2048]), ubf(32 * KB, [2048])]
    kTb = [ubf(4 * KB, [2048]), ubf(36 * KB, [2048])]
    vtokb = [ubf(8 * KB, [2048]), ubf(40 * KB, [2048])]
    Ebuf = [U[:, (12 * KB + i * 4 * KB) // 4:(12 * KB + (i + 1) * 4 * KB) // 4] for i in range(2)]
    Lbuf = [ubf(20 * KB + i * 2 * KB, [1024]) for i in range(2)]
    Lsum = [ubf(24 * KB + i * 2 * KB, [1024]) for i in range(2)]
    Abuf = [ubf(28 * KB + i * 2 * KB, [1024]) for i in range(2)]
    ubuf = [ubf(i * 2112, [1056]) for i in range(2)]
    halo = ubf(52 * KB, [120])
    dgb = [ubf(4608 + i * 7936, [31 * 128]) for i in range(2)]
    cacc = U[:, 20 * KB // 4: 36 * KB // 4]
    mstat = U[:, 36 * KB // 4: 40 * KB // 4]
    rstat = U[:, 40 * KB // 4: 44 * KB // 4]
    agT = ubf(44 * KB, [4096])
    yT = ubf(0, [8192])
    ubo = [ubf(53 * KB + i * 2112, [1056]) for i in range(2)]

    def v3(t, a, n):
        return t[:, a * n:(a + 1) * n]

    for s in range(nseq):
        for kc in range(KC):
            src = xT_d[s, kc]
            sch.dma("sp", f"xld{kc}", lambda e, kc=kc, src=src: e.dma_start(out=xT[:, kc, :], in_=src), (),
                    [("x", kc, tc) for tc in range(4)])
        for li in range(n_layers if DBG != "pro" else 0):
            for tc in range(4):
                tok = slice(tc * 512, (tc + 1) * 512)
                pb, pr = next_bank()
                for kc in range(KC):
                    sq, sqr = rsq()
                    act(sq[:], xT[:, kc, tok], AF.Square, [("x", kc, tc)], [sqr])
                    mm(pb, ON1024, sq[:], kc == 0, kc == KC - 1, [sqr, "cst"], [pr])
                lnv, lnr = rf()
                act(lnv[:], pb, AF.Ln, [pr], [lnr], bias=EPS)
                rstd, rr = rstdN, "rstdN"
                act(rstd[:], lnv[:], AF.Exp, [lnr], [rr], scale=-0.5)
                for kc in range(KC):
                    tmp, tr = rf()
                    gc = (li * 8 + kc) * nseq + s
                    stt(tmp[:], xT[:, kc, tok], g1[:, gc:gc + 1], rstd[:], ALU.mult, ALU.mult,
                        [("x", kc, tc), rr, "g1"], [tr])
                    mc = (li * 24 + kc) * nseq + s
                    act(hT[:, kc, tok], tmp[:], AF.Identity, [tr, "modT"], [("h", kc, tc)], bias=modT[:, mc:mc + 1])
            sch.fence()
            if DBG == "N":
                for kc in range(KC):
                    for tc in range(4):
                        tok = slice(tc * 512, (tc + 1) * 512)
                        cp("dve", xT[:, kc, tok], hT[:, kc, tok], [("h", kc, tc)], [("x", kc, tc)])
                continue
            def proj_sched(hp, attn_banks):
                par = hp % 2
                sched = {}

                def add(b, f):
                    sched.setdefault(b, []).append(f)

                u = 0
                for which, pj, dst, gain in (("q", 12 + hp, qTb[par], qg8[:, li:li + 1]),
                                             ("k", 16 + hp, kTb[par], vcol(li, 33))):
                    holder = {}
                    for tc in range(4):
                        qf, qfr = f512[u % 3], ("f512", u % 3)
                        lnv, lnr = f512[3 + u % 3], ("f512", 3 + u % 3)
                        sq, sqr = sqb[u % 4], ("sq", u % 4)
                        tok = slice(tc * 512, (tc + 1) * 512)

                        def s1(pj=pj, tc=tc, holder=holder, qf=qf, qfr=qfr):
                            if "w" not in holder:
                                holder["w"] = wpiece("in", li, pj, 1024)
                            w, wr = holder["w"]
                            pb, pr = next_bank(attn_banks)
                            proj8(pb, pr, w, wr, tc)
                            cp("dve", qf[:], pb, [pr], [qfr])

                        def s2(qf=qf, qfr=qfr, sq=sq, sqr=sqr):
                            tt("dve", sq[:], qf[:], qf[:], ALU.mult, [qfr], [sqr])

                        def s3(which=which, dst=dst, gain=gain, tc=tc, tok=tok, qf=qf, qfr=qfr, sq=sq, sqr=sqr,
                               lnv=lnv, lnr=lnr):
                            pb2, pr2 = next_bank(attn_banks)
                            mm(pb2, BD64, sq[:], True, True, [sqr, "cst"], [pr2])
                            act(lnv[:], pb2, AF.Ln, [pr2], [lnr], bias=EPS)
                            act(lnv[:], lnv[:], AF.Exp, [lnr], [lnr], scale=-0.5)
                            stt(dst[:, tok], qf[:], gain, lnv[:], ALU.mult, ALU.mult, [qfr, lnr, "qg8", "vecs"],
                                [(which, par, tc)])

                        add(u, s1)
                        add(u + 1, s2)
                        add(u + 2, s3)
                        u += 1
                holder = {}
                for g in range(4):
                    st = {}

                    def v1(g=g, holder=holder, st=st):
                        if "w" not in holder:
                            holder["w"] = wpiece("in", li, 20 + hp, 1024)
                        w, wr = holder["w"]
                        pb, pr = next_bank(attn_banks)
                        st["pb"] = (pb, pr)
                        for j in range(4):
                            tb = g * 4 + j
                            for kc in range(KC):
                                mm(pb[:, j * 128:(j + 1) * 128], hT[:, kc, tb * 128:(tb + 1) * 128],
                                   w[:, kc * 128:(kc + 1) * 128], kc == 0, kc == KC - 1, [wr, ("h", kc, g)], [pr],
                                   inc=(kc == KC - 1 and j == 3))

                    def v2(g=g, st=st):
                        pb, pr = st["pb"]
                        cp("dve", vtokb[par][:, g * 512:(g + 1) * 512], pb, [pr], [("v", par, g)])

                    add(10 + g, v1)
                    add(11 + g, v2)
                return sched

            sc0 = proj_sched(0, False)
            for b_ in sorted(sc0):
                for f_ in sc0[b_]:
                    f_()
            for hp in range(4):
                par = hp % 2
                qT, kT, vtok = qTb[par], kTb[par], vtokb[par]
                nxt_sched = proj_sched(hp + 1, True) if hp < 3 else {}
                bidx = [0]

                steps = [(qc, kb) for qc in range(4) for kb in range(4 * qc + 3, -1, -1)]
                nst = len(steps)
                lsum_par = [0]
                info = {}

                def geom(i):
                    qc, kb = steps[i]
                    diag = kb >= 4 * qc
                    c0 = 128 * (kb - 4 * qc) if diag else 0
                    first = kb == 4 * qc + 3
                    last = kb == 0
                    return qc, kb, diag, c0, first, last

                def emit_qk(i):
                    qc, kb, diag, c0, first, last = geom(i)
                    zi = i % 2
                    for h in range(2):
                        pr_ = slice(64 * h, 64 * h + 64)
                        mm(Z[zi][:, h, c0:512], kT[pr_, kb * 128:(kb + 1) * 128],
                           qT[pr_, qc * 512 + c0:(qc + 1) * 512], True, True,
                           [("k", par, kb // 4), ("q", par, qc)], [("ps", 2 * zi + h)], inc=(h == 1))

                def emit_e1(i):
                    qc, kb, diag, c0, first, last = geom(i)
                    zi = i % 2
                    E = Ebuf[zi].rearrange("p (h n) -> p h n", h=2)
                    act(E[:, :, c0:512], Z[zi][:, :, c0:512], AF.Exp, [("ps", 2 * zi), ("ps", 2 * zi + 1)], [("E", zi)])

                def emit_ln(i):
                    qc, kb, diag, c0, first, last = geom(i)
                    zi = i % 2
                    E = Ebuf[zi].rearrange("p (h n) -> p h n", h=2)
                    L = Lbuf[zi].rearrange("p (h n) -> p h n", h=2)
                    act(L[:, :, c0:512], E[:, :, c0:512], AF.Ln, [("E", zi)], [("L", zi)], bias=1.0)
                    if diag:
                        tt("dve", L[:, :, c0:c0 + 128], L[:, :, c0:c0 + 128],
                           MASK2.rearrange("p (h n) -> p h n", h=2), ALU.mult, [("L", zi), "cst"], [("L", zi)])

                def emit_tri(i):
                    qc, kb, diag, c0, first, last = geom(i)
                    zi = i % 2
                    L = Lbuf[zi].rearrange("p (h n) -> p h n", h=2)
                    cur = lsum_par[0]
                    LS = Lsum[cur].rearrange("p (h n) -> p h n", h=2)
                    cs = c0 + 128 if diag else 0
                    has_car = (not first) and cs < 512
                    for h in range(2):
                        mm(Z[zi][:, h, c0:512], TRI, L[:, h, c0:512], False, not has_car,
                           [("L", zi), "cst"], [("ps", 2 * zi + h)], inc=(h == 1 and not has_car), skip_group_check=True)
                    if has_car:
                        for h in range(2):
                            mm(Z[zi][:, h, cs:512], NONES, LS[:, h, cs:512], False, True,
                               [("Ls", cur), "cst"], [("ps", 2 * zi + h)], inc=(h == 1), skip_group_check=True)
                    if not last:
                        nw = 1 - cur
                        LN_ = Lsum[nw].rearrange("p (h n) -> p h n", h=2)
                        if diag:
                            cp("dve", LN_[:, :, c0:c0 + 128], L[:, :, c0:c0 + 128], [("L", zi)], [("Ls", nw)])
                        if cs < 512 and not first:
                            tt("dve", LN_[:, :, cs:512], LS[:, :, cs:512], L[:, :, cs:512], ALU.add,
                               [("L", zi), ("Ls", cur)], [("Ls", nw)])
                        lsum_par[0] = nw

                def emit_e2(i):
                    qc, kb, diag, c0, first, last = geom(i)
                    zi = i % 2
                    A = Abuf[zi].rearrange("p (h n) -> p h n", h=2)
                    act(A[:, :, c0:512], Z[zi][:, :, c0:512], AF.Exp, [("ps", 2 * zi), ("ps", 2 * zi + 1)], [("A", zi)])
                    if diag:
                        tt("dve", A[:, :, c0:c0 + 128], A[:, :, c0:c0 + 128],
                           MASK2.rearrange("p (h n) -> p h n", h=2), ALU.mult, [("A", zi), "cst"], [("A", zi)])

                def emit_av(i):
                    qc, kb, diag, c0, first, last = geom(i)
                    zi = i % 2
                    oi = qc % 2
                    A = Abuf[zi].rearrange("p (h n) -> p h n", h=2)
                    for h in range(2):
                        mm(Ob[oi][64 * h:64 * h + 64, c0:512], vtok[:, kb * 128 + 64 * h:kb * 128 + 64 * h + 64],
                           A[:, h, c0:512], first, last, [("A", zi), ("v", par, kb // 4)], [("ps", 4 + oi)],
                           inc=(h == 1), skip_group_check=True)
                    if last:
                        cp("dve", oT[:, hp, qc * 512:(qc + 1) * 512], Ob[oi][:], [("ps", 4 + oi)], [("o", hp, qc)])

                for p0 in range(0, nst, 2):
                    pair = [i for i in (p0, p0 + 1) if i < nst]
                    prev = [i - 2 for i in pair if i - 2 >= 0]
                    for k_, i in enumerate(pair):
                        emit_qk(i)
                        if k_ < len(prev):
                            emit_av(prev[k_])
                    for i in pair:
                        emit_e1(i)
                    for i in pair:
                        emit_ln(i)
                    for i in pair:
                        emit_tri(i)
                    for i in pair:
                        emit_e2(i)
                    for f_ in nxt_sched.pop(bidx[0], []):
                        f_()
                    bidx[0] += 1
                for i in range(max(0, nst - 2), nst):
                    emit_av(i)
                for b_ in sorted(nxt_sched):
                    for f_ in nxt_sched[b_]:
                        f_()
            sch.fence()
            if DBG == "A":
                continue
            for tb2 in range(2):
                sch.fence()
                pending = []

                def conv_diag(cc):
                    wbase = 34 + cc * 31
                    dg = dgb[cc % 2]
                    for k in range(31):
                        ts("dve", dg[:, k * 128:(k + 1) * 128], IDENT, vcol(li, wbase + k), None, ALU.mult, None,
                           ["cst", "vecs"], [("dg", cc % 2, k)])

                def conv_cc(cc, ub, ubr):
                    dg = dgb[cc % 2]
                    uo = ubo[cc % 2]
                    uor = ("ubo", cc % 2)
                    act(uo[:, 0:1053], ub[:, 1:1054], AF.Identity, [ubr], [uor])
                    for t in range(2):
                        pc, pcr = next_bank()
                        for k in range(31):
                            src = ub[:, t * 512 + k:t * 512 + k + 512] if k % 2 == 0 else \
                                uo[:, t * 512 + k - 1:t * 512 + k - 1 + 512]
                            mm(pc, dg[:, k * 128:(k + 1) * 128], src, k == 0, k == 30,
                               [("dg", cc % 2, k), ubr, uor], [pcr], inc=(k == 30))
                        acc = cacc[:, cc * 1024 + t * 512: cc * 1024 + (t + 1) * 512]
                        act(acc, pc, AF.Identity, [pcr, "vecs"], [("cacc", cc, t)], bias=vcol(li, 158 + cc))

                for cc in range(4):
                    conv_diag(cc)
                    wa, war = wpiece("in", li, cc, 1024)
                    wb, wbr = wpiece("in", li, 4 + cc, 1024)
                    ub = ubuf[cc % 2]
                    ubr = ("ub", cc % 2)
                    if tb2 == 0:
                        sch.op("pool", lambda e, ub=ub: e.memset(ub[:, 0:30], 0.0), (), [ubr])
                    else:
                        cp("pool", ub[:, 0:30], halo[:, cc * 30:(cc + 1) * 30], [("halo", cc)], [ubr])
                    for t in range(2):
                        tc = tb2 * 2 + t
                        pa, par = next_bank()
                        proj8(pa, par, wa, war, tc)
                        pbk, pbr = next_bank()
                        proj8(pbk, pbr, wb, wbr, tc)
                        sg_, sgr = rf()
                        act(sg_[:], pbk, AF.Sigmoid, [pbr], [sgr])
                        tt("dve", ub[:, 30 + t * 512:30 + (t + 1) * 512], pa, sg_[:], ALU.mult, [par, sgr], [ubr])
                    if tb2 == 0:
                        cp("pool", halo[:, cc * 30:(cc + 1) * 30], ub[:, 1024:1054], [ubr], [("halo", cc)])
                    if pending:
                        conv_cc(*pending.pop())
                    pending.append((cc, ub, ubr))
                conv_cc(*pending.pop())
                for t in range(2):
                    pm, pmr = next_bank()
                    pq, pqr = next_bank()
                    for cc in range(4):
                        acc = cacc[:, cc * 1024 + t * 512: cc * 1024 + (t + 1) * 512]
                        ab, abr = rsq()
                        cp("dve", ab[:], acc, [("cacc", cc, t)], [abr])
                        sq, sqr = rsq()
                        act(sq[:], acc, AF.Square, [("cacc", cc, t)], [sqr])
                        mm(pm, ON512, ab[:], cc == 0, cc == 3, [abr, "cst"], [pmr])
                        mm(pq, ON512, sq[:], cc == 0, cc == 3, [sqr, "cst"], [pqr])
                    mean = mstat[:, t * 512:(t + 1) * 512]
                    cp("dve", mean, pm, [pmr], [("mean", t)])
                    m2, m2r = rf()
                    tt("dve", m2[:], mean, mean, ALU.mult, [("mean", t)], [m2r])
                    var, vr = rf()
                    tt("dve", var[:], pq, m2[:], ALU.subtract, [pqr, m2r], [vr])
                    lnv, lnr = rf()
                    act(lnv[:], var[:], AF.Ln, [vr], [lnr], bias=EPS)
                    act(rstat[:, t * 512:(t + 1) * 512], lnv[:], AF.Exp, [lnr], [("rstd", t)], scale=-0.5)
                for hp in range(4):
                    wg, wgr = wpiece("in", li, 24 + hp, 1024)
                    for t in range(2):
                        tc = tb2 * 2 + t
                        tok = slice(tc * 512, (tc + 1) * 512)
                        pg, pgr = next_bank()
                        proj8(pg, pgr, wg, wgr, tc)
                        s2, s2r = rf()
                        act(s2[:], pg, AF.Sigmoid, [pgr], [s2r])
                        tt("dve", s2[:], pg, s2[:], ALU.mult, [pgr, s2r], [s2r])
                        tt("pool", oT[:, hp, tok], oT[:, hp, tok], s2[:], ALU.mult, [("o", hp, tc), s2r], [("o", hp, tc)])
                for cc in range(4):
                    wg, wgr = wpiece("in", li, 8 + cc, 1024)
                    for t in range(2):
                        tc = tb2 * 2 + t
                        acc = cacc[:, cc * 1024 + t * 512: cc * 1024 + (t + 1) * 512]
                        pg, pgr = next_bank()
                        proj8(pg, pgr, wg, wgr, tc)
                        xc, xcr = rf()
                        tt("dve", xc[:], acc, mstat[:, t * 512:(t + 1) * 512], ALU.subtract,
                           [("cacc", cc, t), ("mean", t)], [xcr])
                        xn, xnr = rf()
                        stt(xn[:], xc[:], vcol(li, 162 + cc), rstat[:, t * 512:(t + 1) * 512], ALU.mult, ALU.mult,
                            [xcr, ("rstd", t), "vecs"], [xnr])
                        s1, s1r = rf()
                        act(s1[:], xn[:], AF.Sigmoid, [xnr, "vecs"], [s1r], bias=vcol(li, 166 + cc))
                        stt(xc[:], xn[:], vcol(li, 166 + cc), s1[:], ALU.add, ALU.mult, [xnr, s1r, "vecs"], [xcr])
                        s2, s2r = rf()
                        act(s2[:], pg, AF.Sigmoid, [pgr], [s2r])
                        tt("dve", s2[:], pg, s2[:], ALU.mult, [pgr, s2r], [s2r])
                        tt("pool", agT[:, cc * 1024 + t * 512: cc * 1024 + (t + 1) * 512], xc[:], s2[:], ALU.mult,
                           [xcr, s2r], [("ag", cc, t)])
                sch.fence()
                for j in range(8):
                    wma, wmar = wpiece("in", li, 28 + j, 1024)
                    wmb, wmbr = wpiece("in", li, 36 + j, 1024)
                    wc, wcr = wpiece("co", li, j, 512)
                    ws, wsr = wpiece("so", li, j, 512)
                    for t in range(2):
                        tc = tb2 * 2 + t
                        tok = slice(tc * 512, (tc + 1) * 512)
                        pma, pmar = next_bank()
                        proj8(pma, pmar, wma, wmar, tc)
                        pmb, pmbr = next_bank()
                        proj8(pmb, pmbr, wmb, wmbr, tc)
                        pya, pyar = next_bank()
                        for kc in range(4):
                            mm(pya, wc[:, kc * 128:(kc + 1) * 128], agT[:, kc * 1024 + t * 512:kc * 1024 + (t + 1) * 512],
                               kc == 0, kc == 3, [wcr, ("ag", kc, t)], [pyar], inc=(kc == 3))
                        pyb, pybr = next_bank()
                        for kc in range(4):
                            mm(pyb, ws[:, kc * 128:(kc + 1) * 128], oT[:, kc, tok], kc == 0, kc == 3,
                               [wsr, ("o", kc, tc)], [pybr], inc=(kc == 3))
                        sa, sar = rf()
                        act(sa[:], pma, AF.Sigmoid, [pmar], [sar])
                        sbb, sbr = rf()
                        act(sbb[:], pmb, AF.Sigmoid, [pmbr], [sbr])
                        tt("dve", sa[:], pya, sa[:], ALU.mult, [pyar, sar], [sar])
                        tt("dve", sbb[:], pyb, sbb[:], ALU.mult, [pybr, sbr], [sbr])
                        tt("pool", yT[:, j * 1024 + t * 512:j * 1024 + (t + 1) * 512], sa[:], sbb[:], ALU.add,
                           [sar, sbr], [("y", j, t)])
                for j in range(8):
                    wo, wor = wpiece("o", li, j, 1024)
                    for t in range(2):
                        tc = tb2 * 2 + t
                        tok = slice(tc * 512, (tc + 1) * 512)
                        po, por = next_bank()
                        for kc in range(KC):
                            mm(po, wo[:, kc * 128:(kc + 1) * 128], yT[:, kc * 1024 + t * 512:kc * 1024 + (t + 1) * 512],
                               kc == 0, kc == KC - 1, [wor, ("y", kc, t)], [por], inc=(kc == KC - 1))
                        gcol = (li * 24 + 16 + j) * nseq + s
                        stt(xT[:, j, tok], po, modT[:, gcol:gcol + 1], xT[:, j, tok], ALU.mult, ALU.add,
                            [por, "modT", ("x", j, tc)], [("x", j, tc)])
            sch.fence()
        for kc in range(KC):
            dst = yT_d[s, kc]
            sch.dma("sp", f"yst{kc}", lambda e, kc=kc, dst=dst: e.dma_start(out=dst, in_=xT[:, kc, :]),
                    [("x", kc, tc) for tc in range(4)], [])
    sch.wait_all("sp")

    semnames = set(sch.cnt.keys())
    sems = {n: es.enter_context(nc.semaphore(n)) for n in sorted(semnames)}
    engobj = {"pe": "tensor", "act": "scalar", "dve": "vector", "pool": "gpsimd", "sp": "sync"}
    with nc.Block() as block:
        for ename, battr in engobj.items():
            def body(eng, ename=ename):
                waited = {}
                for deps, fn, semname, inc in sch.q[ename]:
                    for sname, v in deps.items():
                        if v > waited.get(sname, 0):
                            eng.wait_ge(sems[sname], v)
                            waited[sname] = v
                    if fn is None:
                        continue
                    ins = fn(eng)
                    if semname is not None:
                        ins.then_inc(sems[semname], inc)
            getattr(block, battr)(body)
    es.close()
    return nc


def _consts():
    j = np.arange(128)[:, None]
    s = np.arange(128)[None, :]
    tri = np.where(j >= s, -1.0, 0.0)
    nones = -np.ones((128, 128))
    bd = np.where((j // 64) == (s // 64), 1.0 / 64, 0.0)
    o1024 = np.full((128, 128), 1.0 / 1024)
    o512 = np.full((128, 128), 1.0 / 512)
    mask = np.where(j < s, 1.0, 0.0)
    ident = np.where(j == s, 1.0, 0.0)
    return np.concatenate([tri, nones, bd, o1024, o512, mask, mask, ident], axis=1).astype(np.float32)


def _pieces(w, ncolchunks):
    K, N = w.shape
    kc = K // 128
    return np.ascontiguousarray(w.reshape(kc, 128, N // 128, 128).transpose(2, 1, 0, 3)).reshape(N // 128, 128, kc * 128)


def _layout_weights(layers, w_ada, b_ada, norm_g, w_in, q_gain, k_gain, w_dw, b_dw, ln_g, ln_b,
                    w_conv_out, w_sb_out, w_out):
    nl = len(layers)
    vec = np.zeros((128, nl * VL), np.float32)
    for i, l in enumerate(layers):
        o = i * VL
        vec[:, o:o + 24] = b_ada[l].reshape(24, 128).T
        vec[:, o + 24:o + 32] = norm_g[l].reshape(8, 128).T
        vec[:, o + 32] = np.tile(q_gain[l], 2)
        vec[:, o + 33] = np.tile(k_gain[l], 2)
        vec[:, o + 34:o + 158] = w_dw[l].reshape(31, 4, 128).transpose(2, 1, 0).reshape(128, 124)
        vec[:, o + 158:o + 162] = b_dw[l].reshape(4, 128).T
        vec[:, o + 162:o + 166] = ln_g[l].reshape(4, 128).T
        vec[:, o + 166:o + 170] = ln_b[l].reshape(4, 128).T
    wada = np.stack([_pieces(w_ada[l], 24) for l in layers])
    win = np.stack([_pieces(w_in[l], 44) for l in layers])
    wco = np.stack([_pieces(w_conv_out[l], 8) for l in layers])
    wso = np.stack([_pieces(w_sb_out[l], 8) for l in layers])
    wo = np.stack([_pieces(w_out[l], 8) for l in layers])
    return dict(vecs=vec, wada=wada, win=win, wco=wco, wso=wso, wo=wo, csts=_consts())


_NC_CACHE = {}


def _get_nc(n_layers, nseq):
    key = (n_layers, nseq)
    if key not in _NC_CACHE:
        _NC_CACHE[key] = build(n_layers, nseq)
    return _NC_CACHE[key]


def run_layers(xT_all, c, layers, weights, ncores=NCORES, nseq=NSEQ):
    wl = _layout_weights(layers, **weights)
    nc = _get_nc(len(layers), nseq)
    in_maps = []
    for core in range(ncores):
        b0 = core * nseq
        cT = np.ascontiguousarray(c[b0:b0 + nseq].reshape(nseq, KC, 128).transpose(2, 1, 0)).reshape(128, KC * nseq)
        m = dict(wl)
        m["xT"] = np.ascontiguousarray(xT_all[b0:b0 + nseq])
        m["cT"] = cT.astype(np.float32)
        in_maps.append(m)
    res = run_bass_kernel_spmd(nc, in_maps, core_ids=list(range(ncores)))
    return np.concatenate([r["yT"] for r in res.results], axis=0)


FUSED = True


def kernel(x, c, w_ada, b_ada, norm_g, w_in, q_gain, k_gain, w_dw, b_dw, ln_g, ln_b,
           w_conv_out, w_sb_out, w_out):
    x = np.asarray(x, np.float32)
    c = np.asarray(c, np.float32)
    weights = dict(w_ada=np.asarray(w_ada, np.float32), b_ada=np.asarray(b_ada, np.float32),
                   norm_g=np.asarray(norm_g, np.float32), w_in=np.asarray(w_in, np.float32),
                   q_gain=np.asarray(q_gain, np.float32), k_gain=np.asarray(k_gain, np.float32),
                   w_dw=np.asarray(w_dw, np.float32), b_dw=np.asarray(b_dw, np.float32),
                   ln_g=np.asarray(ln_g, np.float32), ln_b=np.asarray(ln_b, np.float32),
                   w_conv_out=np.asarray(w_conv_out, np.float32), w_sb_out=np.asarray(w_sb_out, np.float32),
                   w_out=np.asarray(w_out, np.float32))
    B = x.shape[0]
    xT = np.ascontiguousarray(x.transpose(0, 2, 1)).reshape(B, KC, 128, S)
    if FUSED:
        xT = run_layers(xT, c, list(range(NL)), weights)
    else:
        for l in range(NL):
            xT = run_layers(xT, c, [l], weights)
    return np.ascontiguousarray(xT.reshape(B, D, S).transpose(0, 2, 1)).astype(np.float32)
```

```python
import contextlib
import numpy as np
import concourse.bass as bass
import concourse.mybir as mybir
from concourse.bass_utils import run_bass_kernel_spmd

F32 = mybir.dt.float32
BF16 = mybir.dt.bfloat16
AF = mybir.ActivationFunctionType
ALU = mybir.AluOpType

D = 1024
S = 2048
KC = 8
NL = 4
DIN = 5632
EPS = 1e-6
NCORES = 8
NSEQ = 2
VL = 24 + 8 + 1 + 1 + 4 * 31 + 4 + 4 + 4
NCST = 128 * 5 + 256 + 128
NSLOT = 6
NF512 = 6


class Sched:
    def __init__(self):
        self.q = {e: [] for e in ("pe", "act", "dve", "pool", "sp")}
        self.cnt = {}
        self.lastw = {}
        self.rd = {}

    def _deps(self, reads, writes, eng):
        deps = {}

        def add(tok):
            if tok is None:
                return
            s, v = tok
            if s == "pe" and eng == "pe":
                return
            if deps.get(s, 0) < v:
                deps[s] = v

        for r in reads:
            add(self.lastw.get(r))
        for w in writes:
            add(self.lastw.get(w))
            for s, v in self.rd.get(w, {}).items():
                add((s, v))
        return deps

    def _commit(self, reads, writes, tok):
        for r in reads:
            d = self.rd.setdefault(r, {})
            if d.get(tok[0], 0) < tok[1]:
                d[tok[0]] = tok[1]
        for w in writes:
            self.lastw[w] = tok
            self.rd[w] = {}

    def op(self, eng, fn, reads=(), writes=(), inc=True):
        deps = self._deps(reads, writes, eng)
        nxt = self.cnt.get(eng, 0) + 1
        if inc:
            self.cnt[eng] = nxt
        self._commit(reads, writes, (eng, nxt))
        self.q[eng].append((deps, fn, eng if inc else None, 1))

    def dma(self, qeng, dsem, fn, reads=(), writes=(), extra=None):
        deps = self._deps(reads, writes, qeng)
        if extra:
            for k_, v_ in extra.items():
                if v_ > deps.get(k_, 0):
                    deps[k_] = v_
        self.cnt[dsem] = self.cnt.get(dsem, 0) + 16
        self._commit(reads, writes, (dsem, self.cnt[dsem]))
        self.q[qeng].append((deps, fn, dsem, 16))

    def fence(self, engines=("pe", "act", "dve", "pool")):
        snap = {k: v for k, v in self.cnt.items() if k in engines}
        for e in engines:
            self.q[e].append((dict(snap), None, None, 0))

    def wait_all(self, eng):
        self.q[eng].append((dict(self.cnt), None, None, 0))


import os
DBG = os.environ.get("KDBG", "")


def build(n_layers, nseq):
    nc = bass.Bass("TRN2", target_bir_lowering=False)
    NV = n_layers * VL
    xT_d = nc.dram_tensor("xT", [nseq, KC, 128, S], F32, kind="ExternalInput").ap()
    yT_d = nc.dram_tensor("yT", [nseq, KC, 128, S], F32, kind="ExternalOutput").ap()
    cT_d = nc.dram_tensor("cT", [128, KC * nseq], F32, kind="ExternalInput").ap()
    vec_d = nc.dram_tensor("vecs", [128, NV], F32, kind="ExternalInput").ap()
    cst_d = nc.dram_tensor("csts", [128, NCST], F32, kind="ExternalInput").ap()
    wada_d = nc.dram_tensor("wada", [n_layers, 24, 128, KC * 128], F32, kind="ExternalInput").ap()
    win_d = nc.dram_tensor("win", [n_layers, 44, 128, KC * 128], F32, kind="ExternalInput").ap()
    wco_d = nc.dram_tensor("wco", [n_layers, 8, 128, 4 * 128], F32, kind="ExternalInput").ap()
    wso_d = nc.dram_tensor("wso", [n_layers, 8, 128, 4 * 128], F32, kind="ExternalInput").ap()
    wo_d = nc.dram_tensor("wo", [n_layers, 8, 128, 8 * 128], F32, kind="ExternalInput").ap()

    wbf_d = {
        "in": nc.dram_tensor("wbf_in", [n_layers, 44, 128, KC * 128], BF16, kind="Internal").ap(),
        "co": nc.dram_tensor("wbf_co", [n_layers, 8, 128, 4 * 128], BF16, kind="Internal").ap(),
        "so": nc.dram_tensor("wbf_so", [n_layers, 8, 128, 4 * 128], BF16, kind="Internal").ap(),
        "o": nc.dram_tensor("wbf_o", [n_layers, 8, 128, 8 * 128], BF16, kind="Internal").ap(),
    }
    wf32_d = {"in": win_d, "co": wco_d, "so": wso_d, "o": wo_d}
    sch = Sched()
    es = contextlib.ExitStack()

    def sb(name, shape, dt):
        return es.enter_context(nc.sbuf_tensor("sb_" + name, shape, dt))

    xT = sb("xT", [128, KC, S], F32)
    hT = sb("hT", [128, KC, S], BF16)
    oT = sb("oT", [128, 4, S], BF16)
    vecs = sb("vecs", [128, NV], F32)
    cst = sb("cst", [128, NCST], BF16)
    cact = sb("cact", [128, KC * nseq], F32)
    cin = sb("cin", [128, KC * nseq], F32)
    modT = sb("modT", [128, n_layers * 24 * nseq], F32)
    g1 = sb("g1", [128, n_layers * 8 * nseq], F32)
    qg8 = sb("qg8", [128, n_layers], F32)
    ring = [sb(f"ring{i}", [128, 1024], BF16) for i in range(NSLOT)]
    f512 = [sb(f"f512_{i}", [128, 512], F32) for i in range(NF512)]
    sqb = [sb(f"sq{i}", [128, 512], BF16) for i in range(4)]
    rstdN = sb("rstdN", [128, 512], F32)
    UBYTES = 58 * 1024
    U = sb("U", [128, UBYTES // 4], F32)

    TRI = cst[:, 0:128]
    NONES = cst[:, 128:256]
    BD64 = cst[:, 256:384]
    ON1024 = cst[:, 384:512]
    ON512 = cst[:, 512:640]
    MASK2 = cst[:, 640:896]
    IDENT = cst[:, 896:1024]

    Z = [es.enter_context(nc.psum_tensor(f"Z{i}", [128, 2, 512], F32)) for i in range(3)]
    Ob = [es.enter_context(nc.psum_tensor(f"O{i}", [128, 512], F32)) for i in range(1)]
    Pb = [es.enter_context(nc.psum_tensor(f"P{i}", [128, 512], F32)) for i in range(1)]
    banks = [Z[0][:, 0, :], Z[0][:, 1, :], Z[1][:, 0, :], Z[1][:, 1, :], Z[2][:, 0, :], Z[2][:, 1, :],
             Ob[0][:], Pb[0][:]]
    bank_ctr = [0, 0]

    def next_bank(attn=False):
        if attn:
            i = 7
        else:
            i = bank_ctr[0] % 8
            bank_ctr[0] += 1
        return banks[i], ("ps", i)

    rot_ctr = {"f": 0, "sq": 0, "ring": 0}

    def rf():
        i = rot_ctr["f"] % NF512
        rot_ctr["f"] += 1
        return f512[i], ("f512", i)

    def rsq():
        i = rot_ctr["sq"] % 4
        rot_ctr["sq"] += 1
        return sqb[i], ("sq", i)

    def mm(out, lhsT, rhs, start, stop, reads, writes, inc=True, **kw):
        sch.op("pe", lambda e: e.matmul(out, lhsT, rhs, start=start, stop=stop, **kw), reads, writes, inc)

    def act(out, in_, func, reads, writes, **kw):
        sch.op("act", lambda e: e.activation(out=out, in_=in_, func=func, **kw), reads, writes)

    def tt(eng, out, in0, in1, op, reads, writes):
        sch.op(eng, lambda e: e.tensor_tensor(out=out, in0=in0, in1=in1, op=op), reads, writes)

    def ts(eng, out, in0, s1, s2, op0, op1, reads, writes):
        if s2 is None:
            sch.op(eng, lambda e: e.tensor_scalar(out=out, in0=in0, scalar1=s1, scalar2=None, op0=op0), reads, writes)
        else:
            sch.op(eng, lambda e: e.tensor_scalar(out=out, in0=in0, scalar1=s1, scalar2=s2, op0=op0, op1=op1), reads, writes)

    def stt(out, in0, scalar, in1, op0, op1, reads, writes):
        sch.op("dve", lambda e: e.scalar_tensor_tensor(out=out, in0=in0, scalar=scalar, in1=in1, op0=op0, op1=op1), reads, writes)

    def cp(eng, out, in_, reads, writes):
        sch.op(eng, lambda e: e.tensor_copy(out=out, in_=in_), reads, writes)

    converted = set()

    def cvgrp(kind, j):
        return "A" if (kind == "in" and 12 <= j < 24) else "B"

    def convert(kind, li, j):
        if (kind, li, j) in converted or li >= n_layers:
            return
        converted.add((kind, li, j))
        src = wf32_d[kind][li, j]
        dst = wbf_d[kind][li, j]
        sch.dma("pool", f"cv{cvgrp(kind, j)}{li}", lambda e: e.dma_start(out=dst, in_=src), (), ())

    def wpiece(kind, li, j, ncols):
        convert(kind, li + 1, j)
        i = rot_ctr["ring"] % NSLOT
        rot_ctr["ring"] += 1
        dst = ring[i][:, 0:ncols]
        src = wbf_d[kind][li, j]
        cs_ = f"cv{cvgrp(kind, j)}{li}"
        sch.dma("sp", f"ring{i}", lambda e: e.dma_start(out=dst, in_=src), (), [("ring", i)],
                extra={cs_: sch.cnt[cs_]})
        return ring[i], ("ring", i)

    def proj8(pb, pr, w, wr, tc):
        for kc in range(KC):
            mm(pb, w[:, kc * 128:(kc + 1) * 128], hT[:, kc, tc * 512:(tc + 1) * 512], kc == 0, kc == KC - 1,
               [wr, ("h", kc, tc)], [pr], inc=(kc == KC - 1))

    def vcol(li, off, n=1):
        c = li * VL + off
        return vecs[:, c:c + n]

    sch.dma("sp", "ldv", lambda e: e.dma_start(out=vecs[:], in_=vec_d), (), ["vecs"])
    sch.dma("pool", "ldc", lambda e: e.dma_start(out=cst[:], in_=cst_d), (), ["cst"])
    for j in list(range(12, 24)) + list(range(0, 12)) + list(range(24, 44)):
        convert("in", 0, j)
    for j in range(8):
        convert("co", 0, j)
        convert("so", 0, j)
    for j in range(8):
        convert("o", 0, j)
    sch.dma("sp", "ldcin", lambda e: e.dma_start(out=cin[:], in_=cT_d), (), ["cin"])
    e0, e0r = rf()
    act(e0[:, 0:KC * nseq], cin[:], AF.Exp, ["cin"], [e0r], scale=-1.0)
    ts("dve", e0[:, 0:KC * nseq], e0[:, 0:KC * nseq], 1.0, None, ALU.add, None, [e0r], [e0r])
    sch.op("dve", lambda e: e.reciprocal(out=e0[:, 0:KC * nseq], in_=e0[:, 0:KC * nseq]), [e0r], [e0r])
    tt("dve", cact[:], cin[:], e0[:, 0:KC * nseq], ALU.mult, ["cin", e0r], ["cact"])
    wad = [U[:, 0:1024], U[:, 1024:2048]]
    cact3 = cact[:]
    for li in range(n_layers):
        for j in range(24):
            wi = (li * 24 + j) % 2
            wsrc = wada_d[li, j]
            wdst = wad[wi]
            sch.dma("sp", f"wad{wi}", lambda e, wdst=wdst, wsrc=wsrc: e.dma_start(out=wdst, in_=wsrc), (), [("wad", wi)])
            pb, pr = next_bank()
            for kc in range(KC):
                mm(pb[:, 0:nseq], wdst[:, kc * 128:(kc + 1) * 128], cact[:, kc * nseq:(kc + 1) * nseq],
                   kc == 0, kc == KC - 1, [("wad", wi), "cact"], [pr], inc=(kc == KC - 1))
            c0 = (li * 24 + j) * nseq
            ts("dve", modT[:, c0:c0 + nseq], pb[:, 0:nseq], vcol(li, j), None, ALU.add, None, [pr, "vecs"], ["modT"])
        for kc in range(KC):
            c0 = (li * 24 + 8 + kc) * nseq
            gc = (li * 8 + kc) * nseq
            ts("dve", g1[:, gc:gc + nseq], modT[:, c0:c0 + nseq], 1.0, vcol(li, 24 + kc), ALU.add, ALU.mult,
               ["modT", "vecs"], ["g1"])
        ts("dve", qg8[:, li:li + 1], vcol(li, 32), 0.125, None, ALU.mult, None, ["vecs"], ["qg8"])
    sch.fence()

    def ubf(off_bytes, shape):
        n = int(np.prod(shape))
        t = U[:, off_bytes // 4: off_bytes // 4 + n // 2].bitcast(BF16)
        return t

    KB = 1024
    qTb = [ubf(0, [2048]), ubf(32 * KB, [2048])]
    kTb = [ubf(4 * KB, [2048]), ubf(36 * KB, [2048])]
    vtokb = [ubf(8 * KB, [2048]), ubf(40 * KB, [2048])]
    Ebuf = [U[:, (12 * KB + i * 4 * KB) // 4:(12 * KB + (i + 1) * 4 * KB) // 4] for i in range(2)]
    Lbuf = [ubf(20 * KB + i * 2 * KB, [1024]) for i in range(2)]
    Lsum = [ubf(24 * KB + i * 2 * KB, [1024]) for i in range(2)]
    Abuf = [ubf(28 * KB + i * 2 * KB, [1024]) for i in range(2)]
    ubuf = [ubf(i * 2112, [1056]) for i in range(2)]
    halo = ubf(52 * KB, [120])
    dgb = [ubf(4608 + i * 7936, [31 * 128]) for i in range(2)]
    cacc = U[:, 20 * KB // 4: 36 * KB // 4]
    mstat = U[:, 36 * KB // 4: 40 * KB // 4]
    rstat = U[:, 40 * KB // 4: 44 * KB // 4]
    agT = ubf(44 * KB, [4096])
    yT = ubf(0, [8192])
    ubo = [ubf(53 * KB + i * 2112, [1056]) for i in range(2)]

    def v3(t, a, n):
        return t[:, a * n:(a + 1) * n]

    for s in range(nseq):
        for kc in range(KC):
            src = xT_d[s, kc]
            sch.dma("sp", f"xld{kc}", lambda e, kc=kc, src=src: e.dma_start(out=xT[:, kc, :], in_=src), (),
                    [("x", kc, tc) for tc in range(4)])
        for li in range(n_layers if DBG != "pro" else 0):
            for tc in range(4):
                tok = slice(tc * 512, (tc + 1) * 512)
                pb, pr = next_bank()
                for kc in range(KC):
                    sq, sqr = rsq()
                    act(sq[:], xT[:, kc, tok], AF.Square, [("x", kc, tc)], [sqr])
                    mm(pb, ON1024, sq[:], kc == 0, kc == KC - 1, [sqr, "cst"], [pr])
                lnv, lnr = rf()
                act(lnv[:], pb, AF.Ln, [pr], [lnr], bias=EPS)
                rstd, rr = rstdN, "rstdN"
                act(rstd[:], lnv[:], AF.Exp, [lnr], [rr], scale=-0.5)
                for kc in range(KC):
                    tmp, tr = rf()
                    gc = (li * 8 + kc) * nseq + s
                    stt(tmp[:], xT[:, kc, tok], g1[:, gc:gc + 1], rstd[:], ALU.mult, ALU.mult,
                        [("x", kc, tc), rr, "g1"], [tr])
                    mc = (li * 24 + kc) * nseq + s
                    act(hT[:, kc, tok], tmp[:], AF.Identity, [tr, "modT"], [("h", kc, tc)], bias=modT[:, mc:mc + 1])
            if DBG == "N":
                sch.fence()
                for kc in range(KC):
                    for tc in range(4):
                        tok = slice(tc * 512, (tc + 1) * 512)
                        cp("dve", xT[:, kc, tok], hT[:, kc, tok], [("h", kc, tc)], [("x", kc, tc)])
                continue
            def proj_sched(hp, attn_banks):
                par = hp % 2
                sched = {}

                def add(b, f, prio):
                    sched.setdefault(b, []).append((prio, f))

                u = 0
                for which, pj, dst, gain in (("q", 12 + hp, qTb[par], qg8[:, li:li + 1]),
                                             ("k", 16 + hp, kTb[par], vcol(li, 33))):
                    holder = {}
                    for tc in range(4):
                        qf, qfr = f512[u % 3], ("f512", u % 3)
                        lnv, lnr = f512[3 + u % 3], ("f512", 3 + u % 3)
                        sq, sqr = sqb[u % 4], ("sq", u % 4)
                        tok = slice(tc * 512, (tc + 1) * 512)

                        def s1(pj=pj, tc=tc, holder=holder, qf=qf, qfr=qfr):
                            if "w" not in holder:
                                holder["w"] = wpiece("in", li, pj, 1024)
                            w, wr = holder["w"]
                            pb, pr = next_bank(attn_banks)
                            proj8(pb, pr, w, wr, tc)
                            cp("dve", qf[:], pb, [pr], [qfr])

                        def s2(qf=qf, qfr=qfr, sq=sq, sqr=sqr):
                            act(sq[:], qf[:], AF.Square, [qfr], [sqr])

                        def s3(which=which, dst=dst, gain=gain, tc=tc, tok=tok, qf=qf, qfr=qfr, sq=sq, sqr=sqr,
                               lnv=lnv, lnr=lnr):
                            pb2, pr2 = next_bank(attn_banks)
                            mm(pb2, BD64, sq[:], True, True, [sqr, "cst"], [pr2])
                            act(lnv[:], pb2, AF.Ln, [pr2], [lnr], bias=EPS)
                            act(lnv[:], lnv[:], AF.Exp, [lnr], [lnr], scale=-0.5)
                            stt(dst[:, tok], qf[:], gain, lnv[:], ALU.mult, ALU.mult, [qfr, lnr, "qg8", "vecs"],
                                [(which, par, tc)])

                        add(4 * u, s1, 1)
                        add(4 * u + 2, s2, 2)
                        add(4 * u + 6, s3, 3)
                        u += 1
                holder = {}
                for g in range(4):
                    st = {}

                    def v1(g=g, holder=holder, st=st):
                        if "w" not in holder:
                            holder["w"] = wpiece("in", li, 20 + hp, 1024)
                        w, wr = holder["w"]
                        pb, pr = next_bank(attn_banks)
                        st["pb"] = (pb, pr)
                        for j in range(4):
                            tb = g * 4 + j
                            for kc in range(KC):
                                mm(pb[:, j * 128:(j + 1) * 128], hT[:, kc, tb * 128:(tb + 1) * 128],
                                   w[:, kc * 128:(kc + 1) * 128], kc == 0, kc == KC - 1, [wr, ("h", kc, g)], [pr],
                                   inc=(kc == KC - 1 and j == 3))

                    def v2(g=g, st=st):
                        pb, pr = st["pb"]
                        cp("dve", vtokb[par][:, g * 512:(g + 1) * 512], pb, [pr], [("v", par, g)])

                    add(1 + 8 * g, v1, 1)
                    add(2 + 8 * g, v2, 0)
                return sched

            sc0 = proj_sched(0, False)
            for b_ in sorted(sc0):
                for _, f_ in sc0[b_]:
                    f_()
            for hp in range(4):
                par = hp % 2
                qT, kT, vtok = qTb[par], kTb[par], vtokb[par]
                nxt_sched = proj_sched(hp + 1, True) if hp < 3 else {}
                bidx = [0]

                steps = [(qc, kb) for qc in range(4) for kb in range(4 * qc + 3, -1, -1)]
                nst = len(steps)
                lsum_par = [0]
                info = {}

                def geom(i):
                    qc, kb = steps[i]
                    diag = kb >= 4 * qc
                    c0 = 128 * (kb - 4 * qc) if diag else 0
                    first = kb == 4 * qc + 3
                    last = kb == 0
                    return qc, kb, diag, c0, first, last

                def emit_qk(i):
                    qc, kb, diag, c0, first, last = geom(i)
                    zi = i % 2
                    zz = i % 3
                    for h in range(2):
                        pr_ = slice(64 * h, 64 * h + 64)
                        mm(Z[zz][:, h, c0:512], kT[pr_, kb * 128:(kb + 1) * 128],
                           qT[pr_, qc * 512 + c0:(qc + 1) * 512], True, True,
                           [("k", par, kb // 4), ("q", par, qc)], [("ps", 2 * zz + h)], inc=(h == 1))

                def emit_e1(i):
                    qc, kb, diag, c0, first, last = geom(i)
                    zi = i % 2
                    zz = i % 3
                    E = Ebuf[zi].rearrange("p (h n) -> p h n", h=2)
                    act(E[:, :, c0:512], Z[zz][:, :, c0:512], AF.Exp, [("ps", 2 * zz), ("ps", 2 * zz + 1)], [("E", zi)])

                def emit_ln(i):
                    qc, kb, diag, c0, first, last = geom(i)
                    zi = i % 2
                    zz = i % 3
                    E = Ebuf[zi].rearrange("p (h n) -> p h n", h=2)
                    L = Lbuf[zi].rearrange("p (h n) -> p h n", h=2)
                    act(L[:, :, c0:512], E[:, :, c0:512], AF.Ln, [("E", zi)], [("L", zi)], bias=1.0)
                    if diag:
                        tt("dve", L[:, :, c0:c0 + 128], L[:, :, c0:c0 + 128],
                           MASK2.rearrange("p (h n) -> p h n", h=2), ALU.mult, [("L", zi), "cst"], [("L", zi)])

                def emit_tri(i):
                    qc, kb, diag, c0, first, last = geom(i)
                    zi = i % 2
                    zz = i % 3
                    L = Lbuf[zi].rearrange("p (h n) -> p h n", h=2)
                    cur = lsum_par[0]
                    LS = Lsum[cur].rearrange("p (h n) -> p h n", h=2)
                    cs = c0 + 128 if diag else 0
                    has_car = (not first) and cs < 512
                    for h in range(2):
                        mm(Z[zz][:, h, c0:512], TRI, L[:, h, c0:512], False, not has_car,
                           [("L", zi), "cst"], [("ps", 2 * zz + h)], inc=(h == 1 and not has_car), skip_group_check=True)
                    if has_car:
                        for h in range(2):
                            mm(Z[zz][:, h, cs:512], NONES, LS[:, h, cs:512], False, True,
                               [("Ls", cur), "cst"], [("ps", 2 * zz + h)], inc=(h == 1), skip_group_check=True)
                    if not last:
                        nw = 1 - cur
                        LN_ = Lsum[nw].rearrange("p (h n) -> p h n", h=2)
                        if diag:
                            cp("dve", LN_[:, :, c0:c0 + 128], L[:, :, c0:c0 + 128], [("L", zi)], [("Ls", nw)])
                        if cs < 512 and not first:
                            tt("dve", LN_[:, :, cs:512], LS[:, :, cs:512], L[:, :, cs:512], ALU.add,
                               [("L", zi), ("Ls", cur)], [("Ls", nw)])
                        lsum_par[0] = nw

                def emit_e2(i):
                    qc, kb, diag, c0, first, last = geom(i)
                    zi = i % 2
                    zz = i % 3
                    A = Abuf[zi].rearrange("p (h n) -> p h n", h=2)
                    act(A[:, :, c0:512], Z[zz][:, :, c0:512], AF.Exp, [("ps", 2 * zz), ("ps", 2 * zz + 1)], [("A", zi)])
                    if diag:
                        tt("dve", A[:, :, c0:c0 + 128], A[:, :, c0:c0 + 128],
                           MASK2.rearrange("p (h n) -> p h n", h=2), ALU.mult, [("A", zi), "cst"], [("A", zi)])

                def emit_av(i):
                    qc, kb, diag, c0, first, last = geom(i)
                    zi = i % 2
                    zz = i % 3
                    oi = 0
                    A = Abuf[zi].rearrange("p (h n) -> p h n", h=2)
                    for h in range(2):
                        mm(Ob[oi][64 * h:64 * h + 64, c0:512], vtok[:, kb * 128 + 64 * h:kb * 128 + 64 * h + 64],
                           A[:, h, c0:512], first, last, [("A", zi), ("v", par, kb // 4)], [("ps", 6)],
                           inc=(h == 1), skip_group_check=True)
                    if last:
                        cp("dve", oT[:, hp, qc * 512:(qc + 1) * 512], Ob[oi][:], [("ps", 6)], [("o", hp, qc)])

                def run_boundary():
                    for _, f_ in sorted(nxt_sched.pop(bidx[0], []), key=lambda pf: pf[0]):
                        f_()
                    bidx[0] += 1

                for t in range(-2, nst + 1):
                    if 0 <= t - 1 < nst:
                        emit_e2(t - 1)
                    if 0 <= t + 2 < nst:
                        emit_qk(t + 2)
                    if 0 <= t - 1 < nst:
                        emit_av(t - 1)
                    if 0 <= t + 1 < nst:
                        emit_ln(t + 1)
                        emit_tri(t + 1)
                    if 0 <= t + 2 < nst:
                        emit_e1(t + 2)
                    run_boundary()
                for b_ in sorted(nxt_sched):
                    for _, f_ in sorted(nxt_sched[b_], key=lambda pf: pf[0]):
                        f_()
            sch.fence()
            if DBG == "A":
                continue
            for tb2 in range(2):
                sch.fence()
                pending = []

                def conv_diag(cc):
                    wbase = 34 + cc * 31
                    dg = dgb[cc % 2]
                    for k in range(31):
                        ts("dve", dg[:, k * 128:(k + 1) * 128], IDENT, vcol(li, wbase + k), None, ALU.mult, None,
                           ["cst", "vecs"], [("dg", cc % 2, k)])

                def conv_cc(cc, ub, ubr):
                    dg = dgb[cc % 2]
                    uo = ubo[cc % 2]
                    uor = ("ubo", cc % 2)
                    act(uo[:, 0:1053], ub[:, 1:1054], AF.Identity, [ubr], [uor])
                    for t in range(2):
                        pc, pcr = next_bank()
                        for k in range(31):
                            src = ub[:, t * 512 + k:t * 512 + k + 512] if k % 2 == 0 else \
                                uo[:, t * 512 + k - 1:t * 512 + k - 1 + 512]
                            mm(pc, dg[:, k * 128:(k + 1) * 128], src, k == 0, k == 30,
                               [("dg", cc % 2, k), ubr, uor], [pcr], inc=(k == 30))
                        acc = cacc[:, cc * 1024 + t * 512: cc * 1024 + (t + 1) * 512]
                        act(acc, pc, AF.Identity, [pcr, "vecs"], [("cacc", cc, t)], bias=vcol(li, 158 + cc))

                for cc in range(4):
                    conv_diag(cc)
                    wa, war = wpiece("in", li, cc, 1024)
                    wb, wbr = wpiece("in", li, 4 + cc, 1024)
                    ub = ubuf[cc % 2]
                    ubr = ("ub", cc % 2)
                    if tb2 == 0:
                        sch.op("pool", lambda e, ub=ub: e.memset(ub[:, 0:30], 0.0), (), [ubr])
                    else:
                        cp("pool", ub[:, 0:30], halo[:, cc * 30:(cc + 1) * 30], [("halo", cc)], [ubr])
                    for t in range(2):
                        tc = tb2 * 2 + t
                        pa, par = next_bank()
                        proj8(pa, par, wa, war, tc)
                        pbk, pbr = next_bank()
                        proj8(pbk, pbr, wb, wbr, tc)
                        sg_, sgr = rf()
                        act(sg_[:], pbk, AF.Sigmoid, [pbr], [sgr])
                        tt("dve", ub[:, 30 + t * 512:30 + (t + 1) * 512], pa, sg_[:], ALU.mult, [par, sgr], [ubr])
                    if tb2 == 0:
                        cp("pool", halo[:, cc * 30:(cc + 1) * 30], ub[:, 1024:1054], [ubr], [("halo", cc)])
                    if pending:
                        conv_cc(*pending.pop())
                    pending.append((cc, ub, ubr))
                conv_cc(*pending.pop())
                for t in range(2):
                    pm, pmr = next_bank()
                    pq, pqr = next_bank()
                    for cc in range(4):
                        acc = cacc[:, cc * 1024 + t * 512: cc * 1024 + (t + 1) * 512]
                        ab, abr = rsq()
                        cp("dve", ab[:], acc, [("cacc", cc, t)], [abr])
                        sq, sqr = rsq()
                        act(sq[:], acc, AF.Square, [("cacc", cc, t)], [sqr])
                        mm(pm, ON512, ab[:], cc == 0, cc == 3, [abr, "cst"], [pmr])
                        mm(pq, ON512, sq[:], cc == 0, cc == 3, [sqr, "cst"], [pqr])
                    mean = mstat[:, t * 512:(t + 1) * 512]
                    cp("dve", mean, pm, [pmr], [("mean", t)])
                    m2, m2r = rf()
                    tt("dve", m2[:], mean, mean, ALU.mult, [("mean", t)], [m2r])
                    var, vr = rf()
                    tt("dve", var[:], pq, m2[:], ALU.subtract, [pqr, m2r], [vr])
                    lnv, lnr = rf()
                    act(lnv[:], var[:], AF.Ln, [vr], [lnr], bias=EPS)
                    act(rstat[:, t * 512:(t + 1) * 512], lnv[:], AF.Exp, [lnr], [("rstd", t)], scale=-0.5)
                for hp in range(4):
                    wg, wgr = wpiece("in", li, 24 + hp, 1024)
                    for t in range(2):
                        tc = tb2 * 2 + t
                        tok = slice(tc * 512, (tc + 1) * 512)
                        pg, pgr = next_bank()
                        proj8(pg, pgr, wg, wgr, tc)
                        s2, s2r = rf()
                        act(s2[:], pg, AF.Sigmoid, [pgr], [s2r])
                        tt("dve", s2[:], pg, s2[:], ALU.mult, [pgr, s2r], [s2r])
                        tt("pool", oT[:, hp, tok], oT[:, hp, tok], s2[:], ALU.mult, [("o", hp, tc), s2r], [("o", hp, tc)])
                for cc in range(4):
                    wg, wgr = wpiece("in", li, 8 + cc, 1024)
                    for t in range(2):
                        tc = tb2 * 2 + t
                        acc = cacc[:, cc * 1024 + t * 512: cc * 1024 + (t + 1) * 512]
                        pg, pgr = next_bank()
                        proj8(pg, pgr, wg, wgr, tc)
                        xc, xcr = rf()
                        tt("dve", xc[:], acc, mstat[:, t * 512:(t + 1) * 512], ALU.subtract,
                           [("cacc", cc, t), ("mean", t)], [xcr])
                        xn, xnr = rf()
                        stt(xn[:], xc[:], vcol(li, 162 + cc), rstat[:, t * 512:(t + 1) * 512], ALU.mult, ALU.mult,
                            [xcr, ("rstd", t), "vecs"], [xnr])
                        s1, s1r = rf()
                        act(s1[:], xn[:], AF.Sigmoid, [xnr, "vecs"], [s1r], bias=vcol(li, 166 + cc))
                        stt(xc[:], xn[:], vcol(li, 166 + cc), s1[:], ALU.add, ALU.mult, [xnr, s1r, "vecs"], [xcr])
                        s2, s2r = rf()
                        act(s2[:], pg, AF.Sigmoid, [pgr], [s2r])
                        tt("dve", s2[:], pg, s2[:], ALU.mult, [pgr, s2r], [s2r])
                        tt("pool", agT[:, cc * 1024 + t * 512: cc * 1024 + (t + 1) * 512], xc[:], s2[:], ALU.mult,
                           [xcr, s2r], [("ag", cc, t)])
                sch.fence()
                for j in range(8):
                    wma, wmar = wpiece("in", li, 28 + j, 1024)
                    wmb, wmbr = wpiece("in", li, 36 + j, 1024)
                    wc, wcr = wpiece("co", li, j, 512)
                    ws, wsr = wpiece("so", li, j, 512)
                    for t in range(2):
                        tc = tb2 * 2 + t
                        tok = slice(tc * 512, (tc + 1) * 512)
                        pma, pmar = next_bank()
                        proj8(pma, pmar, wma, wmar, tc)
                        pmb, pmbr = next_bank()
                        proj8(pmb, pmbr, wmb, wmbr, tc)
                        pya, pyar = next_bank()
                        for kc in range(4):
                            mm(pya, wc[:, kc * 128:(kc + 1) * 128], agT[:, kc * 1024 + t * 512:kc * 1024 + (t + 1) * 512],
                               kc == 0, kc == 3, [wcr, ("ag", kc, t)], [pyar], inc=(kc == 3))
                        pyb, pybr = next_bank()
                        for kc in range(4):
                            mm(pyb, ws[:, kc * 128:(kc + 1) * 128], oT[:, kc, tok], kc == 0, kc == 3,
                               [wsr, ("o", kc, tc)], [pybr], inc=(kc == 3))
                        sa, sar = rf()
                        act(sa[:], pma, AF.Sigmoid, [pmar], [sar])
                        sbb, sbr = rf()
                        act(sbb[:], pmb, AF.Sigmoid, [pmbr], [sbr])
                        tt("dve", sa[:], pya, sa[:], ALU.mult, [pyar, sar], [sar])
                        tt("dve", sbb[:], pyb, sbb[:], ALU.mult, [pybr, sbr], [sbr])
                        tt("pool", yT[:, j * 1024 + t * 512:j * 1024 + (t + 1) * 512], sa[:], sbb[:], ALU.add,
                           [sar, sbr], [("y", j, t)])
                for j in range(8):
                    wo, wor = wpiece("o", li, j, 1024)
                    for t in range(2):
                        tc = tb2 * 2 + t
                        tok = slice(tc * 512, (tc + 1) * 512)
                        po, por = next_bank()
                        for kc in range(KC):
                            mm(po, wo[:, kc * 128:(kc + 1) * 128], yT[:, kc * 1024 + t * 512:kc * 1024 + (t + 1) * 512],
                               kc == 0, kc == KC - 1, [wor, ("y", kc, t)], [por], inc=(kc == KC - 1))
                        gcol = (li * 24 + 16 + j) * nseq + s
                        stt(xT[:, j, tok], po, modT[:, gcol:gcol + 1], xT[:, j, tok], ALU.mult, ALU.add,
                            [por, "modT", ("x", j, tc)], [("x", j, tc)])
            sch.fence()
        for kc in range(KC):
            dst = yT_d[s, kc]
            sch.dma("sp", f"yst{kc}", lambda e, kc=kc, dst=dst: e.dma_start(out=dst, in_=xT[:, kc, :]),
                    [("x", kc, tc) for tc in range(4)], [])
    sch.wait_all("sp")

    semnames = set(sch.cnt.keys())
    sems = {n: es.enter_context(nc.semaphore(n)) for n in sorted(semnames)}
    engobj = {"pe": "tensor", "act": "scalar", "dve": "vector", "pool": "gpsimd", "sp": "sync"}
    with nc.Block() as block:
        for ename, battr in engobj.items():
            def body(eng, ename=ename):
                waited = {}
                for deps, fn, semname, inc in sch.q[ename]:
                    for sname, v in deps.items():
                        if v > waited.get(sname, 0):
                            eng.wait_ge(sems[sname], v)
                            waited[sname] = v
                    if fn is None:
                        continue
                    ins = fn(eng)
                    if semname is not None:
                        ins.then_inc(sems[semname], inc)
            getattr(block, battr)(body)
    es.close()
    return nc


def _consts():
    j = np.arange(128)[:, None]
    s = np.arange(128)[None, :]
    tri = np.where(j >= s, -1.0, 0.0)
    nones = -np.ones((128, 128))
    bd = np.where((j // 64) == (s // 64), 1.0 / 64, 0.0)
    o1024 = np.full((128, 128), 1.0 / 1024)
    o512 = np.full((128, 128), 1.0 / 512)
    mask = np.where(j < s, 1.0, 0.0)
    ident = np.where(j == s, 1.0, 0.0)
    return np.concatenate([tri, nones, bd, o1024, o512, mask, mask, ident], axis=1).astype(np.float32)


def _pieces(w, ncolchunks):
    K, N = w.shape
    kc = K // 128
    return np.ascontiguousarray(w.reshape(kc, 128, N // 128, 128).transpose(2, 1, 0, 3)).reshape(N // 128, 128, kc * 128)


def _layout_weights(layers, w_ada, b_ada, norm_g, w_in, q_gain, k_gain, w_dw, b_dw, ln_g, ln_b,
                    w_conv_out, w_sb_out, w_out):
    nl = len(layers)
    vec = np.zeros((128, nl * VL), np.float32)
    for i, l in enumerate(layers):
        o = i * VL
        vec[:, o:o + 24] = b_ada[l].reshape(24, 128).T
        vec[:, o + 24:o + 32] = norm_g[l].reshape(8, 128).T
        vec[:, o + 32] = np.tile(q_gain[l], 2)
        vec[:, o + 33] = np.tile(k_gain[l], 2)
        vec[:, o + 34:o + 158] = w_dw[l].reshape(31, 4, 128).transpose(2, 1, 0).reshape(128, 124)
        vec[:, o + 158:o + 162] = b_dw[l].reshape(4, 128).T
        vec[:, o + 162:o + 166] = ln_g[l].reshape(4, 128).T
        vec[:, o + 166:o + 170] = ln_b[l].reshape(4, 128).T
    wada = np.stack([_pieces(w_ada[l], 24) for l in layers])
    win = np.stack([_pieces(w_in[l], 44) for l in layers])
    wco = np.stack([_pieces(w_conv_out[l], 8) for l in layers])
    wso = np.stack([_pieces(w_sb_out[l], 8) for l in layers])
    wo = np.stack([_pieces(w_out[l], 8) for l in layers])
    return dict(vecs=vec, wada=wada, win=win, wco=wco, wso=wso, wo=wo, csts=_consts())


_NC_CACHE = {}


def _get_nc(n_layers, nseq):
    key = (n_layers, nseq)
    if key not in _NC_CACHE:
        _NC_CACHE[key] = build(n_layers, nseq)
    return _NC_CACHE[key]


def run_layers(xT_all, c, layers, weights, ncores=NCORES, nseq=NSEQ):
    wl = _layout_weights(layers, **weights)
    nc = _get_nc(len(layers), nseq)
    in_maps = []
    for core in range(ncores):
        b0 = core * nseq
        cT = np.ascontiguousarray(c[b0:b0 + nseq].reshape(nseq, KC, 128).transpose(2, 1, 0)).reshape(128, KC * nseq)
        m = dict(wl)
        m["xT"] = np.ascontiguousarray(xT_all[b0:b0 + nseq])
        m["cT"] = cT.astype(np.float32)
        in_maps.append(m)
    res = run_bass_kernel_spmd(nc, in_maps, core_ids=list(range(ncores)))
    return np.concatenate([r["yT"] for r in res.results], axis=0)


FUSED = True


def kernel(x, c, w_ada, b_ada, norm_g, w_in, q_gain, k_gain, w_dw, b_dw, ln_g, ln_b,
           w_conv_out, w_sb_out, w_out):
    x = np.asarray(x, np.float32)
    c = np.asarray(c, np.float32)
    weights = dict(w_ada=np.asarray(w_ada, np.float32), b_ada=np.asarray(b_ada, np.float32),
                   norm_g=np.asarray(norm_g, np.float32), w_in=np.asarray(w_in, np.float32),
                   q_gain=np.asarray(q_gain, np.float32), k_gain=np.asarray(k_gain, np.float32),
                   w_dw=np.asarray(w_dw, np.float32), b_dw=np.asarray(b_dw, np.float32),
                   ln_g=np.asarray(ln_g, np.float32), ln_b=np.asarray(ln_b, np.float32),
                   w_conv_out=np.asarray(w_conv_out, np.float32), w_sb_out=np.asarray(w_sb_out, np.float32),
                   w_out=np.asarray(w_out, np.float32))
    B = x.shape[0]
    xT = np.ascontiguousarray(x.transpose(0, 2, 1)).reshape(B, KC, 128, S)
    if FUSED:
        xT = run_layers(xT, c, list(range(NL)), weights)
    else:
        for l in range(NL):
            xT = run_layers(xT, c, [l], weights)
    return np.ascontiguousarray(xT.reshape(B, D, S).transpose(0, 2, 1)).astype(np.float32)
```

```python
import contextlib
import numpy as np
import concourse.bass as bass
import concourse.mybir as mybir
from concourse.bass_utils import run_bass_kernel_spmd

F32 = mybir.dt.float32
BF16 = mybir.dt.bfloat16
AF = mybir.ActivationFunctionType
ALU = mybir.AluOpType

D = 1024
S = 2048
KC = 8
NL = 4
DIN = 5632
EPS = 1e-6
NCORES = 8
NSEQ = 2
VL = 24 + 8 + 1 + 1 + 4 * 31 + 4 + 4 + 4
NCST = 128 * 5 + 256 + 128
NSLOT = 6
NF512 = 6


class Sched:
    def __init__(self):
        self.q = {e: [] for e in ("pe", "act", "dve", "pool", "sp")}
        self.cnt = {}
        self.lastw = {}
        self.rd = {}

    def _deps(self, reads, writes, eng):
        deps = {}

        def add(tok):
            if tok is None:
                return
            s, v = tok
            if s == "pe" and eng == "pe":
                return
            if deps.get(s, 0) < v:
                deps[s] = v

        for r in reads:
            add(self.lastw.get(r))
        for w in writes:
            add(self.lastw.get(w))
            for s, v in self.rd.get(w, {}).items():
                add((s, v))
        return deps

    def _commit(self, reads, writes, tok):
        for r in reads:
            d = self.rd.setdefault(r, {})
            if d.get(tok[0], 0) < tok[1]:
                d[tok[0]] = tok[1]
        for w in writes:
            self.lastw[w] = tok
            self.rd[w] = {}

    def op(self, eng, fn, reads=(), writes=(), inc=True):
        deps = self._deps(reads, writes, eng)
        nxt = self.cnt.get(eng, 0) + 1
        if inc:
            self.cnt[eng] = nxt
        self._commit(reads, writes, (eng, nxt))
        self.q[eng].append((deps, fn, eng if inc else None, 1))

    def dma(self, qeng, dsem, fn, reads=(), writes=(), extra=None):
        deps = self._deps(reads, writes, qeng)
        if extra:
            for k_, v_ in extra.items():
                if v_ > deps.get(k_, 0):
                    deps[k_] = v_
        self.cnt[dsem] = self.cnt.get(dsem, 0) + 16
        self._commit(reads, writes, (dsem, self.cnt[dsem]))
        self.q[qeng].append((deps, fn, dsem, 16))

    def fence(self, engines=("pe", "act", "dve", "pool")):
        snap = {k: v for k, v in self.cnt.items() if k in engines}
        for e in engines:
            self.q[e].append((dict(snap), None, None, 0))

    def wait_all(self, eng):
        self.q[eng].append((dict(self.cnt), None, None, 0))


import os
DBG = os.environ.get("KDBG", "")


def build(n_layers, nseq):
    nc = bass.Bass("TRN2", target_bir_lowering=False)
    NV = n_layers * VL
    xT_d = nc.dram_tensor("xT", [nseq, KC, 128, S], F32, kind="ExternalInput").ap()
    yT_d = nc.dram_tensor("yT", [nseq, KC, 128, S], F32, kind="ExternalOutput").ap()
    cT_d = nc.dram_tensor("cT", [128, KC * nseq], F32, kind="ExternalInput").ap()
    vec_d = nc.dram_tensor("vecs", [128, NV], F32, kind="ExternalInput").ap()
    cst_d = nc.dram_tensor("csts", [128, NCST], F32, kind="ExternalInput").ap()
    wada_d = nc.dram_tensor("wada", [n_layers, 24, 128, KC * 128], F32, kind="ExternalInput").ap()
    win_d = nc.dram_tensor("win", [n_layers, 44, 128, KC * 128], F32, kind="ExternalInput").ap()
    wco_d = nc.dram_tensor("wco", [n_layers, 8, 128, 4 * 128], F32, kind="ExternalInput").ap()
    wso_d = nc.dram_tensor("wso", [n_layers, 8, 128, 4 * 128], F32, kind="ExternalInput").ap()
    wo_d = nc.dram_tensor("wo", [n_layers, 8, 128, 8 * 128], F32, kind="ExternalInput").ap()

    wbf_d = {
        "in": nc.dram_tensor("wbf_in", [n_layers, 44, 128, KC * 128], BF16, kind="Internal").ap(),
        "co": nc.dram_tensor("wbf_co", [n_layers, 8, 128, 4 * 128], BF16, kind="Internal").ap(),
        "so": nc.dram_tensor("wbf_so", [n_layers, 8, 128, 4 * 128], BF16, kind="Internal").ap(),
        "o": nc.dram_tensor("wbf_o", [n_layers, 8, 128, 8 * 128], BF16, kind="Internal").ap(),
    }
    wf32_d = {"in": win_d, "co": wco_d, "so": wso_d, "o": wo_d}
    sch = Sched()
    es = contextlib.ExitStack()

    def sb(name, shape, dt):
        return es.enter_context(nc.sbuf_tensor("sb_" + name, shape, dt))

    xT = sb("xT", [128, KC, S], F32)
    hT = sb("hT", [128, KC, S], BF16)
    oT = sb("oT", [128, 4, S], BF16)
    vecs = sb("vecs", [128, NV], F32)
    cst = sb("cst", [128, NCST], BF16)
    cact = sb("cact", [128, KC * nseq], F32)
    cin = sb("cin", [128, KC * nseq], F32)
    modT = sb("modT", [128, n_layers * 24 * nseq], F32)
    g1 = sb("g1", [128, n_layers * 8 * nseq], F32)
    qg8 = sb("qg8", [128, n_layers], F32)
    ring = [sb(f"ring{i}", [128, 1024], BF16) for i in range(NSLOT)]
    f512 = [sb(f"f512_{i}", [128, 512], F32) for i in range(NF512)]
    sqb = [sb(f"sq{i}", [128, 512], BF16) for i in range(4)]
    rstdN = sb("rstdN", [128, 512], F32)
    UBYTES = 58 * 1024
    U = sb("U", [128, UBYTES // 4], F32)

    TRI = cst[:, 0:128]
    NONES = cst[:, 128:256]
    BD64 = cst[:, 256:384]
    ON1024 = cst[:, 384:512]
    ON512 = cst[:, 512:640]
    MASK2 = cst[:, 640:896]
    IDENT = cst[:, 896:1024]

    Z = [es.enter_context(nc.psum_tensor(f"Z{i}", [128, 2, 512], F32)) for i in range(3)]
    Ob = [es.enter_context(nc.psum_tensor(f"O{i}", [128, 512], F32)) for i in range(1)]
    Pb = [es.enter_context(nc.psum_tensor(f"P{i}", [128, 512], F32)) for i in range(1)]
    banks = [Z[0][:, 0, :], Z[0][:, 1, :], Z[1][:, 0, :], Z[1][:, 1, :], Z[2][:, 0, :], Z[2][:, 1, :],
             Ob[0][:], Pb[0][:]]
    bank_ctr = [0, 0]

    def next_bank(attn=False):
        if attn:
            i = 7
        else:
            i = bank_ctr[0] % 8
            bank_ctr[0] += 1
        return banks[i], ("ps", i)

    rot_ctr = {"f": 0, "sq": 0, "ring": 0}

    def rf():
        i = rot_ctr["f"] % NF512
        rot_ctr["f"] += 1
        return f512[i], ("f512", i)

    def rsq():
        i = rot_ctr["sq"] % 4
        rot_ctr["sq"] += 1
        return sqb[i], ("sq", i)

    def mm(out, lhsT, rhs, start, stop, reads, writes, inc=True, **kw):
        sch.op("pe", lambda e: e.matmul(out, lhsT, rhs, start=start, stop=stop, **kw), reads, writes, inc)

    def act(out, in_, func, reads, writes, **kw):
        sch.op("act", lambda e: e.activation(out=out, in_=in_, func=func, **kw), reads, writes)

    def tt(eng, out, in0, in1, op, reads, writes):
        sch.op(eng, lambda e: e.tensor_tensor(out=out, in0=in0, in1=in1, op=op), reads, writes)

    def ts(eng, out, in0, s1, s2, op0, op1, reads, writes):
        if s2 is None:
            sch.op(eng, lambda e: e.tensor_scalar(out=out, in0=in0, scalar1=s1, scalar2=None, op0=op0), reads, writes)
        else:
            sch.op(eng, lambda e: e.tensor_scalar(out=out, in0=in0, scalar1=s1, scalar2=s2, op0=op0, op1=op1), reads, writes)

    def stt(out, in0, scalar, in1, op0, op1, reads, writes):
        sch.op("dve", lambda e: e.scalar_tensor_tensor(out=out, in0=in0, scalar=scalar, in1=in1, op0=op0, op1=op1), reads, writes)

    def cp(eng, out, in_, reads, writes):
        sch.op(eng, lambda e: e.tensor_copy(out=out, in_=in_), reads, writes)

    converted = set()

    def cvgrp(kind, j):
        return "A" if (kind == "in" and 12 <= j < 24) else "B"

    def convert(kind, li, j):
        if (kind, li, j) in converted or li >= n_layers:
            return
        converted.add((kind, li, j))
        src = wf32_d[kind][li, j]
        dst = wbf_d[kind][li, j]
        sch.dma("pool", f"cv{cvgrp(kind, j)}{li}", lambda e: e.dma_start(out=dst, in_=src), (), ())

    def wpiece(kind, li, j, ncols):
        convert(kind, li + 1, j)
        i = rot_ctr["ring"] % NSLOT
        rot_ctr["ring"] += 1
        dst = ring[i][:, 0:ncols]
        src = wbf_d[kind][li, j]
        cs_ = f"cv{cvgrp(kind, j)}{li}"
        sch.dma("sp", f"ring{i}", lambda e: e.dma_start(out=dst, in_=src), (), [("ring", i)],
                extra={cs_: sch.cnt[cs_]})
        return ring[i], ("ring", i)

    def proj8(pb, pr, w, wr, tc):
        for kc in range(KC):
            mm(pb, w[:, kc * 128:(kc + 1) * 128], hT[:, kc, tc * 512:(tc + 1) * 512], kc == 0, kc == KC - 1,
               [wr, ("h", kc, tc)], [pr], inc=(kc == KC - 1))

    def vcol(li, off, n=1):
        c = li * VL + off
        return vecs[:, c:c + n]

    sch.dma("sp", "ldv", lambda e: e.dma_start(out=vecs[:], in_=vec_d), (), ["vecs"])
    sch.dma("pool", "ldc", lambda e: e.dma_start(out=cst[:], in_=cst_d), (), ["cst"])
    for j in list(range(12, 24)) + list(range(0, 12)) + list(range(24, 44)):
        convert("in", 0, j)
    for j in range(8):
        convert("co", 0, j)
        convert("so", 0, j)
    for j in range(8):
        convert("o", 0, j)
    sch.dma("sp", "ldcin", lambda e: e.dma_start(out=cin[:], in_=cT_d), (), ["cin"])
    e0, e0r = rf()
    act(e0[:, 0:KC * nseq], cin[:], AF.Exp, ["cin"], [e0r], scale=-1.0)
    ts("dve", e0[:, 0:KC * nseq], e0[:, 0:KC * nseq], 1.0, None, ALU.add, None, [e0r], [e0r])
    sch.op("dve", lambda e: e.reciprocal(out=e0[:, 0:KC * nseq], in_=e0[:, 0:KC * nseq]), [e0r], [e0r])
    tt("dve", cact[:], cin[:], e0[:, 0:KC * nseq], ALU.mult, ["cin", e0r], ["cact"])
    wad = [U[:, 0:1024], U[:, 1024:2048]]
    cact3 = cact[:]
    for li in range(n_layers):
        for j in range(24):
            wi = (li * 24 + j) % 2
            wsrc = wada_d[li, j]
            wdst = wad[wi]
            sch.dma("sp", f"wad{wi}", lambda e, wdst=wdst, wsrc=wsrc: e.dma_start(out=wdst, in_=wsrc), (), [("wad", wi)])
            pb, pr = next_bank()
            for kc in range(KC):
                mm(pb[:, 0:nseq], wdst[:, kc * 128:(kc + 1) * 128], cact[:, kc * nseq:(kc + 1) * nseq],
                   kc == 0, kc == KC - 1, [("wad", wi), "cact"], [pr], inc=(kc == KC - 1))
            c0 = (li * 24 + j) * nseq
            ts("dve", modT[:, c0:c0 + nseq], pb[:, 0:nseq], vcol(li, j), None, ALU.add, None, [pr, "vecs"], ["modT"])
        for kc in range(KC):
            c0 = (li * 24 + 8 + kc) * nseq
            gc = (li * 8 + kc) * nseq
            ts("dve", g1[:, gc:gc + nseq], modT[:, c0:c0 + nseq], 1.0, vcol(li, 24 + kc), ALU.add, ALU.mult,
               ["modT", "vecs"], ["g1"])
        ts("dve", qg8[:, li:li + 1], vcol(li, 32), 0.125, None, ALU.mult, None, ["vecs"], ["qg8"])
    sch.fence()

    def ubf(off_bytes, shape):
        n = int(np.prod(shape))
        t = U[:, off_bytes // 4: off_bytes // 4 + n // 2].bitcast(BF16)
        return t

    KB = 1024
    qTb = [ubf(0, [2048]), ubf(32 * KB, [2048])]
    kTb = [ubf(4 * KB, [2048]), ubf(36 * KB, [2048])]
    vtokb = [ubf(8 * KB, [2048]), ubf(40 * KB, [2048])]
    Ebuf = [U[:, (12 * KB + i * 4 * KB) // 4:(12 * KB + (i + 1) * 4 * KB) // 4] for i in range(2)]
    Lbuf = [ubf(20 * KB + i * 2 * KB, [1024]) for i in range(2)]
    Lsum = [ubf(24 * KB + i * 2 * KB, [1024]) for i in range(2)]
    Abuf = [ubf(28 * KB + i * 2 * KB, [1024]) for i in range(2)]
    ubuf = [ubf(i * 2112, [1056]) for i in range(2)]
    halo = ubf(52 * KB, [120])
    dgb = [ubf(4608 + i * 7936, [31 * 128]) for i in range(2)]
    cacc = U[:, 20 * KB // 4: 36 * KB // 4]
    mstat = U[:, 36 * KB // 4: 40 * KB // 4]
    rstat = U[:, 40 * KB // 4: 44 * KB // 4]
    agT = ubf(44 * KB, [4096])
    yT = ubf(20 * KB, [8192])
    ubo = [ubf(53 * KB + i * 2112, [1056]) for i in range(2)]

    def v3(t, a, n):
        return t[:, a * n:(a + 1) * n]

    for s in range(nseq):
        for kc in range(KC):
            src = xT_d[s, kc]
            sch.dma("sp", f"xld{kc}", lambda e, kc=kc, src=src: e.dma_start(out=xT[:, kc, :], in_=src), (),
                    [("x", kc, tc) for tc in range(4)])
        for li in range(n_layers if DBG != "pro" else 0):
            for tc in range(4):
                tok = slice(tc * 512, (tc + 1) * 512)
                pb, pr = next_bank()
                for kc in range(KC):
                    sq, sqr = rsq()
                    act(sq[:], xT[:, kc, tok], AF.Square, [("x", kc, tc)], [sqr])
                    mm(pb, ON1024, sq[:], kc == 0, kc == KC - 1, [sqr, "cst"], [pr])
                lnv, lnr = rf()
                act(lnv[:], pb, AF.Ln, [pr], [lnr], bias=EPS)
                rstd, rr = rstdN, "rstdN"
                act(rstd[:], lnv[:], AF.Exp, [lnr], [rr], scale=-0.5)
                for kc in range(KC):
                    tmp, tr = rf()
                    gc = (li * 8 + kc) * nseq + s
                    stt(tmp[:], xT[:, kc, tok], g1[:, gc:gc + 1], rstd[:], ALU.mult, ALU.mult,
                        [("x", kc, tc), rr, "g1"], [tr])
                    mc = (li * 24 + kc) * nseq + s
                    act(hT[:, kc, tok], tmp[:], AF.Identity, [tr, "modT"], [("h", kc, tc)], bias=modT[:, mc:mc + 1])
            if DBG == "N":
                sch.fence()
                for kc in range(KC):
                    for tc in range(4):
                        tok = slice(tc * 512, (tc + 1) * 512)
                        cp("dve", xT[:, kc, tok], hT[:, kc, tok], [("h", kc, tc)], [("x", kc, tc)])
                continue
            def proj_sched(hp, attn_banks):
                par = hp % 2
                sched = {}

                def add(b, f, prio):
                    sched.setdefault(b, []).append((prio, f))

                u = 0
                for which, pj, dst, gain in (("q", 12 + hp, qTb[par], qg8[:, li:li + 1]),
                                             ("k", 16 + hp, kTb[par], vcol(li, 33))):
                    holder = {}
                    for tc in range(4):
                        qf, qfr = f512[u % 3], ("f512", u % 3)
                        lnv, lnr = f512[3 + u % 3], ("f512", 3 + u % 3)
                        sq, sqr = sqb[u % 4], ("sq", u % 4)
                        tok = slice(tc * 512, (tc + 1) * 512)

                        def s1(pj=pj, tc=tc, holder=holder, qf=qf, qfr=qfr):
                            if "w" not in holder:
                                holder["w"] = wpiece("in", li, pj, 1024)
                            w, wr = holder["w"]
                            pb, pr = next_bank(attn_banks)
                            proj8(pb, pr, w, wr, tc)
                            cp("dve", qf[:], pb, [pr], [qfr])

                        def s2(qf=qf, qfr=qfr, sq=sq, sqr=sqr):
                            act(sq[:], qf[:], AF.Square, [qfr], [sqr])

                        def s3(which=which, dst=dst, gain=gain, tc=tc, tok=tok, qf=qf, qfr=qfr, sq=sq, sqr=sqr,
                               lnv=lnv, lnr=lnr):
                            pb2, pr2 = next_bank(attn_banks)
                            mm(pb2, BD64, sq[:], True, True, [sqr, "cst"], [pr2])
                            act(lnv[:], pb2, AF.Ln, [pr2], [lnr], bias=EPS)
                            act(lnv[:], lnv[:], AF.Exp, [lnr], [lnr], scale=-0.5)
                            stt(dst[:, tok], qf[:], gain, lnv[:], ALU.mult, ALU.mult, [qfr, lnr, "qg8", "vecs"],
                                [(which, par, tc)])

                        add(4 * u, s1, 1)
                        add(4 * u + 2, s2, 2)
                        add(4 * u + 6, s3, 3)
                        u += 1
                holder = {}
                for g in range(4):
                    st = {}

                    def v1(g=g, holder=holder, st=st):
                        if "w" not in holder:
                            holder["w"] = wpiece("in", li, 20 + hp, 1024)
                        w, wr = holder["w"]
                        pb, pr = next_bank(attn_banks)
                        st["pb"] = (pb, pr)
                        for j in range(4):
                            tb = g * 4 + j
                            for kc in range(KC):
                                mm(pb[:, j * 128:(j + 1) * 128], hT[:, kc, tb * 128:(tb + 1) * 128],
                                   w[:, kc * 128:(kc + 1) * 128], kc == 0, kc == KC - 1, [wr, ("h", kc, g)], [pr],
                                   inc=(kc == KC - 1 and j == 3))

                    def v2(g=g, st=st):
                        pb, pr = st["pb"]
                        cp("dve", vtokb[par][:, g * 512:(g + 1) * 512], pb, [pr], [("v", par, g)])

                    add(1 + 8 * g, v1, 1)
                    add(2 + 8 * g, v2, 0)
                return sched

            sc0 = proj_sched(0, False)
            for b_ in sorted(sc0):
                for _, f_ in sc0[b_]:
                    f_()
            for hp in range(4):
                par = hp % 2
                qT, kT, vtok = qTb[par], kTb[par], vtokb[par]
                nxt_sched = proj_sched(hp + 1, True) if hp < 3 else {}
                bidx = [0]

                steps = [(qc, kb) for qc in range(4) for kb in range(4 * qc + 3, -1, -1)]
                nst = len(steps)
                lsum_par = [0]
                info = {}

                def geom(i):
                    qc, kb = steps[i]
                    diag = kb >= 4 * qc
                    c0 = 128 * (kb - 4 * qc) if diag else 0
                    first = kb == 4 * qc + 3
                    last = kb == 0
                    return qc, kb, diag, c0, first, last

                def emit_qk(i):
                    qc, kb, diag, c0, first, last = geom(i)
                    zi = i % 2
                    zz = i % 3
                    for h in range(2):
                        pr_ = slice(64 * h, 64 * h + 64)
                        mm(Z[zz][:, h, c0:512], kT[pr_, kb * 128:(kb + 1) * 128],
                           qT[pr_, qc * 512 + c0:(qc + 1) * 512], True, True,
                           [("k", par, kb // 4), ("q", par, qc)], [("ps", 2 * zz + h)], inc=(h == 1))

                def emit_e1(i):
                    qc, kb, diag, c0, first, last = geom(i)
                    zi = i % 2
                    zz = i % 3
                    E = Ebuf[zi].rearrange("p (h n) -> p h n", h=2)
                    act(E[:, :, c0:512], Z[zz][:, :, c0:512], AF.Exp, [("ps", 2 * zz), ("ps", 2 * zz + 1)], [("E", zi)])

                def emit_ln(i):
                    qc, kb, diag, c0, first, last = geom(i)
                    zi = i % 2
                    zz = i % 3
                    E = Ebuf[zi].rearrange("p (h n) -> p h n", h=2)
                    L = Lbuf[zi].rearrange("p (h n) -> p h n", h=2)
                    act(L[:, :, c0:512], E[:, :, c0:512], AF.Ln, [("E", zi)], [("L", zi)], bias=1.0)
                    if diag:
                        tt("dve", L[:, :, c0:c0 + 128], L[:, :, c0:c0 + 128],
                           MASK2.rearrange("p (h n) -> p h n", h=2), ALU.mult, [("L", zi), "cst"], [("L", zi)])

                def emit_tri(i):
                    qc, kb, diag, c0, first, last = geom(i)
                    zi = i % 2
                    zz = i % 3
                    L = Lbuf[zi].rearrange("p (h n) -> p h n", h=2)
                    cur = lsum_par[0]
                    LS = Lsum[cur].rearrange("p (h n) -> p h n", h=2)
                    cs = c0 + 128 if diag else 0
                    has_car = (not first) and cs < 512
                    for h in range(2):
                        mm(Z[zz][:, h, c0:512], TRI, L[:, h, c0:512], False, not has_car,
                           [("L", zi), "cst"], [("ps", 2 * zz + h)], inc=(h == 1 and not has_car), skip_group_check=True)
                    if has_car:
                        for h in range(2):
                            mm(Z[zz][:, h, cs:512], NONES, LS[:, h, cs:512], False, True,
                               [("Ls", cur), "cst"], [("ps", 2 * zz + h)], inc=(h == 1), skip_group_check=True)
                    if not last:
                        nw = 1 - cur
                        LN_ = Lsum[nw].rearrange("p (h n) -> p h n", h=2)
                        if diag:
                            cp("dve", LN_[:, :, c0:c0 + 128], L[:, :, c0:c0 + 128], [("L", zi)], [("Ls", nw)])
                        if cs < 512 and not first:
                            tt("dve", LN_[:, :, cs:512], LS[:, :, cs:512], L[:, :, cs:512], ALU.add,
                               [("L", zi), ("Ls", cur)], [("Ls", nw)])
                        lsum_par[0] = nw

                def emit_e2(i):
                    qc, kb, diag, c0, first, last = geom(i)
                    zi = i % 2
                    zz = i % 3
                    A = Abuf[zi].rearrange("p (h n) -> p h n", h=2)
                    act(A[:, :, c0:512], Z[zz][:, :, c0:512], AF.Exp, [("ps", 2 * zz), ("ps", 2 * zz + 1)], [("A", zi)])
                    if diag:
                        tt("dve", A[:, :, c0:c0 + 128], A[:, :, c0:c0 + 128],
                           MASK2.rearrange("p (h n) -> p h n", h=2), ALU.mult, [("A", zi), "cst"], [("A", zi)])

                def emit_av(i):
                    qc, kb, diag, c0, first, last = geom(i)
                    zi = i % 2
                    zz = i % 3
                    oi = 0
                    A = Abuf[zi].rearrange("p (h n) -> p h n", h=2)
                    for h in range(2):
                        mm(Ob[oi][64 * h:64 * h + 64, c0:512], vtok[:, kb * 128 + 64 * h:kb * 128 + 64 * h + 64],
                           A[:, h, c0:512], first, last, [("A", zi), ("v", par, kb // 4)], [("ps", 6)],
                           inc=(h == 1), skip_group_check=True)
                    if last:
                        cp("dve", oT[:, hp, qc * 512:(qc + 1) * 512], Ob[oi][:], [("ps", 6)], [("o", hp, qc)])

                def run_boundary():
                    for _, f_ in sorted(nxt_sched.pop(bidx[0], []), key=lambda pf: pf[0]):
                        f_()
                    bidx[0] += 1

                for t in range(-2, nst + 1):
                    if 0 <= t - 1 < nst:
                        emit_e2(t - 1)
                    if 0 <= t + 2 < nst:
                        emit_qk(t + 2)
                    if 0 <= t - 1 < nst:
                        emit_av(t - 1)
                    if 0 <= t + 1 < nst:
                        emit_ln(t + 1)
                        emit_tri(t + 1)
                    if 0 <= t + 2 < nst:
                        emit_e1(t + 2)
                    run_boundary()
                for b_ in sorted(nxt_sched):
                    for _, f_ in sorted(nxt_sched[b_], key=lambda pf: pf[0]):
                        f_()
            sch.fence()
            if DBG == "A":
                continue
            for tb2 in range(2):
                pending = []

                def conv_diag(cc):
                    wbase = 34 + cc * 31
                    dg = dgb[cc % 2]
                    for k in range(31):
                        ts("dve", dg[:, k * 128:(k + 1) * 128], IDENT, vcol(li, wbase + k), None, ALU.mult, None,
                           ["cst", "vecs"], [("dg", cc % 2, k)])

                def conv_cc(cc, ub, ubr):
                    dg = dgb[cc % 2]
                    uo = ubo[cc % 2]
                    uor = ("ubo", cc % 2)
                    act(uo[:, 0:1053], ub[:, 1:1054], AF.Identity, [ubr], [uor])
                    for t in range(2):
                        pc, pcr = next_bank()
                        for k in range(31):
                            src = ub[:, t * 512 + k:t * 512 + k + 512] if k % 2 == 0 else \
                                uo[:, t * 512 + k - 1:t * 512 + k - 1 + 512]
                            mm(pc, dg[:, k * 128:(k + 1) * 128], src, k == 0, k == 30,
                               [("dg", cc % 2, k), ubr, uor], [pcr], inc=(k == 30))
                        acc = cacc[:, cc * 1024 + t * 512: cc * 1024 + (t + 1) * 512]
                        act(acc, pc, AF.Identity, [pcr, "vecs"], [("cacc", cc, t)], bias=vcol(li, 158 + cc))

                for cc in range(4):
                    conv_diag(cc)
                    wa, war = wpiece("in", li, cc, 1024)
                    wb, wbr = wpiece("in", li, 4 + cc, 1024)
                    ub = ubuf[cc % 2]
                    ubr = ("ub", cc % 2)
                    if tb2 == 0:
                        sch.op("pool", lambda e, ub=ub: e.memset(ub[:, 0:30], 0.0), (), [ubr])
                    else:
                        cp("pool", ub[:, 0:30], halo[:, cc * 30:(cc + 1) * 30], [("halo", cc)], [ubr])
                    for t in range(2):
                        tc = tb2 * 2 + t
                        pa, par = next_bank()
                        proj8(pa, par, wa, war, tc)
                        pbk, pbr = next_bank()
                        proj8(pbk, pbr, wb, wbr, tc)
                        sg_, sgr = rf()
                        act(sg_[:], pbk, AF.Sigmoid, [pbr], [sgr])
                        tt("dve", ub[:, 30 + t * 512:30 + (t + 1) * 512], pa, sg_[:], ALU.mult, [par, sgr], [ubr])
                    if tb2 == 0:
                        cp("pool", halo[:, cc * 30:(cc + 1) * 30], ub[:, 1024:1054], [ubr], [("halo", cc)])
                    if pending:
                        conv_cc(*pending.pop())
                    pending.append((cc, ub, ubr))
                conv_cc(*pending.pop())
                for t in range(2):
                    pm, pmr = next_bank()
                    pq, pqr = next_bank()
                    for cc in range(4):
                        acc = cacc[:, cc * 1024 + t * 512: cc * 1024 + (t + 1) * 512]
                        ab, abr = rsq()
                        cp("dve", ab[:], acc, [("cacc", cc, t)], [abr])
                        sq, sqr = rsq()
                        act(sq[:], acc, AF.Square, [("cacc", cc, t)], [sqr])
                        mm(pm, ON512, ab[:], cc == 0, cc == 3, [abr, "cst"], [pmr])
                        mm(pq, ON512, sq[:], cc == 0, cc == 3, [sqr, "cst"], [pqr])
                    mean = mstat[:, t * 512:(t + 1) * 512]
                    cp("dve", mean, pm, [pmr], [("mean", t)])
                    m2, m2r = rf()
                    tt("dve", m2[:], mean, mean, ALU.mult, [("mean", t)], [m2r])
                    var, vr = rf()
                    tt("dve", var[:], pq, m2[:], ALU.subtract, [pqr, m2r], [vr])
                    lnv, lnr = rf()
                    act(lnv[:], var[:], AF.Ln, [vr], [lnr], bias=EPS)
                    act(rstat[:, t * 512:(t + 1) * 512], lnv[:], AF.Exp, [lnr], [("rstd", t)], scale=-0.5)
                for hp in range(4):
                    wg, wgr = wpiece("in", li, 24 + hp, 1024)
                    for t in range(2):
                        tc = tb2 * 2 + t
                        tok = slice(tc * 512, (tc + 1) * 512)
                        pg, pgr = next_bank()
                        proj8(pg, pgr, wg, wgr, tc)
                        s2, s2r = rf()
                        act(s2[:], pg, AF.Sigmoid, [pgr], [s2r])
                        tt("dve", s2[:], pg, s2[:], ALU.mult, [pgr, s2r], [s2r])
                        tt("pool", oT[:, hp, tok], oT[:, hp, tok], s2[:], ALU.mult, [("o", hp, tc), s2r], [("o", hp, tc)])
                for cc in range(4):
                    wg, wgr = wpiece("in", li, 8 + cc, 1024)
                    for t in range(2):
                        tc = tb2 * 2 + t
                        acc = cacc[:, cc * 1024 + t * 512: cc * 1024 + (t + 1) * 512]
                        pg, pgr = next_bank()
                        proj8(pg, pgr, wg, wgr, tc)
                        xc, xcr = rf()
                        tt("dve", xc[:], acc, mstat[:, t * 512:(t + 1) * 512], ALU.subtract,
                           [("cacc", cc, t), ("mean", t)], [xcr])
                        xn, xnr = rf()
                        stt(xn[:], xc[:], vcol(li, 162 + cc), rstat[:, t * 512:(t + 1) * 512], ALU.mult, ALU.mult,
                            [xcr, ("rstd", t), "vecs"], [xnr])
                        s1, s1r = rf()
                        act(s1[:], xn[:], AF.Sigmoid, [xnr, "vecs"], [s1r], bias=vcol(li, 166 + cc))
                        stt(xc[:], xn[:], vcol(li, 166 + cc), s1[:], ALU.add, ALU.mult, [xnr, s1r, "vecs"], [xcr])
                        s2, s2r = rf()
                        act(s2[:], pg, AF.Sigmoid, [pgr], [s2r])
                        tt("dve", s2[:], pg, s2[:], ALU.mult, [pgr, s2r], [s2r])
                        tt("pool", agT[:, cc * 1024 + t * 512: cc * 1024 + (t + 1) * 512], xc[:], s2[:], ALU.mult,
                           [xcr, s2r], [("ag", cc, t)])
                for j in range(8):
                    wma, wmar = wpiece("in", li, 28 + j, 1024)
                    wmb, wmbr = wpiece("in", li, 36 + j, 1024)
                    wc, wcr = wpiece("co", li, j, 512)
                    ws, wsr = wpiece("so", li, j, 512)
                    for t in range(2):
                        tc = tb2 * 2 + t
                        tok = slice(tc * 512, (tc + 1) * 512)
                        pma, pmar = next_bank()
                        proj8(pma, pmar, wma, wmar, tc)
                        pmb, pmbr = next_bank()
                        proj8(pmb, pmbr, wmb, wmbr, tc)
                        pya, pyar = next_bank()
                        for kc in range(4):
                            mm(pya, wc[:, kc * 128:(kc + 1) * 128], agT[:, kc * 1024 + t * 512:kc * 1024 + (t + 1) * 512],
                               kc == 0, kc == 3, [wcr, ("ag", kc, t)], [pyar], inc=(kc == 3))
                        pyb, pybr = next_bank()
                        for kc in range(4):
                            mm(pyb, ws[:, kc * 128:(kc + 1) * 128], oT[:, kc, tok], kc == 0, kc == 3,
                               [wsr, ("o", kc, tc)], [pybr], inc=(kc == 3))
                        sa, sar = rf()
                        act(sa[:], pma, AF.Sigmoid, [pmar], [sar])
                        sbb, sbr = rf()
                        act(sbb[:], pmb, AF.Sigmoid, [pmbr], [sbr])
                        tt("dve", sa[:], pya, sa[:], ALU.mult, [pyar, sar], [sar])
                        tt("dve", sbb[:], pyb, sbb[:], ALU.mult, [pybr, sbr], [sbr])
                        tt("pool", yT[:, j * 1024 + t * 512:j * 1024 + (t + 1) * 512], sa[:], sbb[:], ALU.add,
                           [sar, sbr], [("y", j, t), ("cacc", j // 2, j % 2)])
                for j in range(8):
                    wo, wor = wpiece("o", li, j, 1024)
                    for t in range(2):
                        tc = tb2 * 2 + t
                        tok = slice(tc * 512, (tc + 1) * 512)
                        po, por = next_bank()
                        for kc in range(KC):
                            mm(po, wo[:, kc * 128:(kc + 1) * 128], yT[:, kc * 1024 + t * 512:kc * 1024 + (t + 1) * 512],
                               kc == 0, kc == KC - 1, [wor, ("y", kc, t), ("cacc", kc // 2, kc % 2)], [por], inc=(kc == KC - 1))
                        gcol = (li * 24 + 16 + j) * nseq + s
                        stt(xT[:, j, tok], po, modT[:, gcol:gcol + 1], xT[:, j, tok], ALU.mult, ALU.add,
                            [por, "modT", ("x", j, tc)], [("x", j, tc)])
            sch.fence()
        for kc in range(KC):
            dst = yT_d[s, kc]
            sch.dma("sp", f"yst{kc}", lambda e, kc=kc, dst=dst: e.dma_start(out=dst, in_=xT[:, kc, :]),
                    [("x", kc, tc) for tc in range(4)], [])
    sch.wait_all("sp")

    semnames = set(sch.cnt.keys())
    sems = {n: es.enter_context(nc.semaphore(n)) for n in sorted(semnames)}
    engobj = {"pe": "tensor", "act": "scalar", "dve": "vector", "pool": "gpsimd", "sp": "sync"}
    with nc.Block() as block:
        for ename, battr in engobj.items():
            def body(eng, ename=ename):
                waited = {}
                for deps, fn, semname, inc in sch.q[ename]:
                    for sname, v in deps.items():
                        if v > waited.get(sname, 0):
                            eng.wait_ge(sems[sname], v)
                            waited[sname] = v
                    if fn is None:
                        continue
                    ins = fn(eng)
                    if semname is not None:
                        ins.then_inc(sems[semname], inc)
            getattr(block, battr)(body)
    es.close()
    return nc


def _consts():
    j = np.arange(128)[:, None]
    s = np.arange(128)[None, :]
    tri = np.where(j >= s, -1.0, 0.0)
    nones = -np.ones((128, 128))
    bd = np.where((j // 64) == (s // 64), 1.0 / 64, 0.0)
    o1024 = np.full((128, 128), 1.0 / 1024)
    o512 = np.full((128, 128), 1.0 / 512)
    mask = np.where(j < s, 1.0, 0.0)
    ident = np.where(j == s, 1.0, 0.0)
    return np.concatenate([tri, nones, bd, o1024, o512, mask, mask, ident], axis=1).astype(np.float32)


def _pieces(w, ncolchunks):
    K, N = w.shape
    kc = K // 128
    return np.ascontiguousarray(w.reshape(kc, 128, N // 128, 128).transpose(2, 1, 0, 3)).reshape(N // 128, 128, kc * 128)


def _layout_weights(layers, w_ada, b_ada, norm_g, w_in, q_gain, k_gain, w_dw, b_dw, ln_g, ln_b,
                    w_conv_out, w_sb_out, w_out):
    nl = len(layers)
    vec = np.zeros((128, nl * VL), np.float32)
    for i, l in enumerate(layers):
        o = i * VL
        vec[:, o:o + 24] = b_ada[l].reshape(24, 128).T
        vec[:, o + 24:o + 32] = norm_g[l].reshape(8, 128).T
        vec[:, o + 32] = np.tile(q_gain[l], 2)
        vec[:, o + 33] = np.tile(k_gain[l], 2)
        vec[:, o + 34:o + 158] = w_dw[l].reshape(31, 4, 128).transpose(2, 1, 0).reshape(128, 124)
        vec[:, o + 158:o + 162] = b_dw[l].reshape(4, 128).T
        vec[:, o + 162:o + 166] = ln_g[l].reshape(4, 128).T
        vec[:, o + 166:o + 170] = ln_b[l].reshape(4, 128).T
    wada = np.stack([_pieces(w_ada[l], 24) for l in layers])
    win = np.stack([_pieces(w_in[l], 44) for l in layers])
    wco = np.stack([_pieces(w_conv_out[l], 8) for l in layers])
    wso = np.stack([_pieces(w_sb_out[l], 8) for l in layers])
    wo = np.stack([_pieces(w_out[l], 8) for l in layers])
    return dict(vecs=vec, wada=wada, win=win, wco=wco, wso=wso, wo=wo, csts=_consts())


_NC_CACHE = {}


def _get_nc(n_layers, nseq):
    key = (n_layers, nseq)
    if key not in _NC_CACHE:
        _NC_CACHE[key] = build(n_layers, nseq)
    return _NC_CACHE[key]


def run_layers(xT_all, c, layers, weights, ncores=NCORES, nseq=NSEQ):
    wl = _layout_weights(layers, **weights)
    nc = _get_nc(len(layers), nseq)
    in_maps = []
    for core in range(ncores):
        b0 = core * nseq
        cT = np.ascontiguousarray(c[b0:b0 + nseq].reshape(nseq, KC, 128).transpose(2, 1, 0)).reshape(128, KC * nseq)
        m = dict(wl)
        m["xT"] = np.ascontiguousarray(xT_all[b0:b0 + nseq])
        m["cT"] = cT.astype(np.float32)
        in_maps.append(m)
    res = run_bass_kernel_spmd(nc, in_maps, core_ids=list(range(ncores)))
    return np.concatenate([r["yT"] for r in res.results], axis=0)


FUSED = True


def kernel(x, c, w_ada, b_ada, norm_g, w_in, q_gain, k_gain, w_dw, b_dw, ln_g, ln_b,
           w_conv_out, w_sb_out, w_out):
    x = np.asarray(x, np.float32)
    c = np.asarray(c, np.float32)
    weights = dict(w_ada=np.asarray(w_ada, np.float32), b_ada=np.asarray(b_ada, np.float32),
                   norm_g=np.asarray(norm_g, np.float32), w_in=np.asarray(w_in, np.float32),
                   q_gain=np.asarray(q_gain, np.float32), k_gain=np.asarray(k_gain, np.float32),
                   w_dw=np.asarray(w_dw, np.float32), b_dw=np.asarray(b_dw, np.float32),
                   ln_g=np.asarray(ln_g, np.float32), ln_b=np.asarray(ln_b, np.float32),
                   w_conv_out=np.asarray(w_conv_out, np.float32), w_sb_out=np.asarray(w_sb_out, np.float32),
                   w_out=np.asarray(w_out, np.float32))
    B = x.shape[0]
    xT = np.ascontiguousarray(x.transpose(0, 2, 1)).reshape(B, KC, 128, S)
    if FUSED:
        xT = run_layers(xT, c, list(range(NL)), weights)
    else:
        for l in range(NL):
            xT = run_layers(xT, c, [l], weights)
    return np.ascontiguousarray(xT.reshape(B, D, S).transpose(0, 2, 1)).astype(np.float32)
```
